# Optimizing a Trainium2 kernel written in Bass

```python
import math
import jax, jax.numpy as jnp
from jax import lax
import numpy as np

D_MODEL = 1024
BATCH = 8
SEQ = 2048
DEPTH = 1

N_HEADS = 8
HEAD_DIM = 128
ATTN_WIDTH = N_HEADS * HEAD_DIM
CONV_WIDTH = D_MODEL
CONV_K = 3
MOBA_BLOCK = 256
MOBA_TOPK = 3
QUERY_CHUNK = 64
D_FF = 2816
LN_EPS = 1e-5
ALPHA = (2.0 * DEPTH) ** 0.25
BETA = (8.0 * DEPTH) ** -0.25
SPLITS = (ATTN_WIDTH, 2 * ATTN_WIDTH, 3 * ATTN_WIDTH,
          3 * ATTN_WIDTH + CONV_WIDTH, 3 * ATTN_WIDTH + 2 * CONV_WIDTH,
          3 * ATTN_WIDTH + 3 * CONV_WIDTH)
PROJ_COLS = 3 * ATTN_WIDTH + 3 * CONV_WIDTH + 2 * D_MODEL

kernel_name = "hybrid_moba_shortconv_macaron_deepnorm"


def layer_norm(x, g, b):
    xf = x.astype(jnp.float32)
    mu = jnp.mean(xf, axis=-1, keepdims=True)
    var = jnp.mean(jnp.square(xf - mu), axis=-1, keepdims=True)
    return ((xf - mu) * lax.rsqrt(var + LN_EPS) * g + b).astype(x.dtype)


def swiglu(x, w_up, w_down):
    gate, up = jnp.split(x @ w_up, 2, axis=-1)
    return (jax.nn.silu(gate) * up) @ w_down


def moba_attention(q, k, v):
    B, H, S, hd = q.shape
    nb = -(-S // MOBA_BLOCK)
    s_pad = nb * MOBA_BLOCK
    pad = [(0, 0), (0, 0), (0, s_pad - S), (0, 0)]
    q, k, v = (jnp.pad(t, pad) for t in (q, k, v))
    scale = hd ** -0.5
    kb = k.reshape(B, H, nb, MOBA_BLOCK, hd)
    vb = v.reshape(B, H, nb, MOBA_BLOCK, hd)
    k_mean = jnp.mean(kb.astype(jnp.float32), axis=3)
    gate = jnp.einsum('bhsd,bhnd->bhsn', q.astype(jnp.float32), k_mean)
    q_blk = jnp.arange(s_pad) // MOBA_BLOCK
    past = jnp.arange(nb)[None, :] < q_blk[:, None]
    gate = jnp.where(past, gate, -jnp.inf)
    n_slots = max(1, min(MOBA_TOPK, nb))
    _, sel = lax.top_k(gate, n_slots)
    sel = sel.astype(jnp.int32)

    n_chunks = s_pad // QUERY_CHUNK
    q_items = q.reshape(B, H, n_chunks, QUERY_CHUNK, hd).transpose(0, 2, 1, 3, 4)
    q_items = q_items.reshape(B * n_chunks, H, QUERY_CHUNK, hd)
    idx_items = sel.reshape(B, H, n_chunks, QUERY_CHUNK, n_slots).transpose(0, 2, 1, 3, 4)
    idx_items = idx_items.reshape(B * n_chunks, H, QUERY_CHUNK, n_slots)
    b_ids = jnp.repeat(jnp.arange(B, dtype=jnp.int32), n_chunks)
    c_ids = jnp.tile(jnp.arange(n_chunks, dtype=jnp.int32), B)

    def one_chunk(args):
        qc, idx, b, c = args
        k_b = lax.dynamic_index_in_dim(kb, b, 0, keepdims=False)
        v_b = lax.dynamic_index_in_dim(vb, b, 0, keepdims=False)
        q_start = c * QUERY_CHUNK
        blk = q_start // MOBA_BLOCK
        k_own = lax.dynamic_index_in_dim(k_b, blk, 1, keepdims=False)
        v_own = lax.dynamic_index_in_dim(v_b, blk, 1, keepdims=False)
        k_sel = jax.vmap(lambda kh, ih: kh[ih])(k_b, idx)
        v_sel = jax.vmap(lambda vh, ih: vh[ih])(v_b, idx)
        s_sel = jnp.einsum('hqd,hqnkd->hqnk', qc, k_sel).astype(jnp.float32) * scale
        slot_ok = jnp.arange(n_slots) < blk
        s_sel = jnp.where(slot_ok[None, None, :, None], s_sel, -jnp.inf)
        s_sel = s_sel.reshape(H, QUERY_CHUNK, n_slots * MOBA_BLOCK)
        s_own = jnp.einsum('hqd,hkd->hqk', qc, k_own).astype(jnp.float32) * scale
        q_pos = q_start + jnp.arange(QUERY_CHUNK)
        k_pos = blk * MOBA_BLOCK + jnp.arange(MOBA_BLOCK)
        own_ok = k_pos[None, :] <= q_pos[:, None]
        s_own = jnp.where(own_ok[None], s_own, -jnp.inf)
        p = jax.nn.softmax(jnp.concatenate([s_sel, s_own], axis=-1), axis=-1)
        p_sel = p[..., :n_slots * MOBA_BLOCK].reshape(H, QUERY_CHUNK, n_slots, MOBA_BLOCK)
        p_own = p[..., n_slots * MOBA_BLOCK:]
        out = (jnp.einsum('hqnk,hqnkd->hqd', p_sel.astype(v_sel.dtype), v_sel)
               + jnp.einsum('hqk,hkd->hqd', p_own.astype(v_own.dtype), v_own))
        return out.astype(qc.dtype)

    out = lax.map(one_chunk, (q_items, idx_items, b_ids, c_ids))
    out = out.reshape(B, n_chunks, H, QUERY_CHUNK, hd).transpose(0, 1, 3, 2, 4)
    return out.reshape(B, s_pad, H * hd)[:, :S]


def short_conv(u, w, bias):
    y = lax.conv_general_dilated(
        u, w[:, None, :], window_strides=(1,), padding=[(CONV_K - 1, 0)],
        dimension_numbers=('NWC', 'WIO', 'NWC'), feature_group_count=u.shape[-1])
    return y + bias


def hybrid_mixer(x, w_in, b_gate, conv_w, conv_b, w_proj_attn, w_proj_conv, w_out):
    B, S, _ = x.shape
    z = x @ w_in
    q, k, v, h, g_b, g_c, gates = jnp.split(z, SPLITS, axis=-1)

    def heads(t):
        return t.reshape(B, S, N_HEADS, HEAD_DIM).transpose(0, 2, 1, 3)

    y_attn = moba_attention(heads(q), heads(k), heads(v)) @ w_proj_attn
    y_conv = (g_b * short_conv(g_c * h, conv_w, conv_b)) @ w_proj_conv
    gate_attn, gate_conv = jnp.split(jax.nn.sigmoid(gates + b_gate), 2, axis=-1)
    return (gate_attn * y_attn + gate_conv * y_conv) @ w_out


def setup_inputs(seed: int = 0) -> dict:
    key = jax.random.key(seed)
    ks = jax.random.split(key, 20)
    nrm = jax.random.normal
    L = DEPTH
    x = nrm(ks[0], (BATCH, SEQ, D_MODEL), jnp.float32)
    ffn1_w_up = nrm(ks[1], (L, D_MODEL, 2 * D_FF), jnp.float32) * D_MODEL ** -0.5
    ffn1_w_down = nrm(ks[2], (L, D_FF, D_MODEL), jnp.float32) * (D_FF ** -0.5 * BETA)
    ln1_g = 1.0 + 0.02 * nrm(ks[3], (L, D_MODEL), jnp.float32)
    ln1_b = 0.02 * nrm(ks[4], (L, D_MODEL), jnp.float32)
    col_scale = jnp.ones((PROJ_COLS,), jnp.float32).at[2 * ATTN_WIDTH:3 * ATTN_WIDTH].set(BETA)
    w_in = nrm(ks[5], (L, D_MODEL, PROJ_COLS), jnp.float32) * D_MODEL ** -0.5 * col_scale
    b_gate = 0.1 * nrm(ks[6], (L, 2 * D_MODEL), jnp.float32)
    conv_w = nrm(ks[7], (L, CONV_K, CONV_WIDTH), jnp.float32) * CONV_K ** -0.5
    conv_b = 0.02 * nrm(ks[8], (L, CONV_WIDTH), jnp.float32)
    w_proj_attn = nrm(ks[9], (L, ATTN_WIDTH, D_MODEL), jnp.float32) * ATTN_WIDTH ** -0.5
    w_proj_conv = nrm(ks[10], (L, CONV_WIDTH, D_MODEL), jnp.float32) * CONV_WIDTH ** -0.5
    w_out = nrm(ks[11], (L, D_MODEL, D_MODEL), jnp.float32) * (D_MODEL ** -0.5 * BETA)
    ln2_g = 1.0 + 0.02 * nrm(ks[12], (L, D_MODEL), jnp.float32)
    ln2_b = 0.02 * nrm(ks[13], (L, D_MODEL), jnp.float32)
    ffn2_w_up = nrm(ks[14], (L, D_MODEL, 2 * D_FF), jnp.float32) * D_MODEL ** -0.5
    ffn2_w_down = nrm(ks[15], (L, D_FF, D_MODEL), jnp.float32) * (D_FF ** -0.5 * BETA)
    ln3_g = 1.0 + 0.02 * nrm(ks[16], (L, D_MODEL), jnp.float32)
    ln3_b = 0.02 * nrm(ks[17], (L, D_MODEL), jnp.float32)
    return {"x": x, "ffn1_w_up": ffn1_w_up, "ffn1_w_down": ffn1_w_down,
            "ln1_g": ln1_g, "ln1_b": ln1_b, "w_in": w_in, "b_gate": b_gate,
            "conv_w": conv_w, "conv_b": conv_b, "w_proj_attn": w_proj_attn,
            "w_proj_conv": w_proj_conv, "w_out": w_out, "ln2_g": ln2_g, "ln2_b": ln2_b,
            "ffn2_w_up": ffn2_w_up, "ffn2_w_down": ffn2_w_down,
            "ln3_g": ln3_g, "ln3_b": ln3_b}


def reference(x, ffn1_w_up, ffn1_w_down, ln1_g, ln1_b, w_in, b_gate, conv_w, conv_b,
              w_proj_attn, w_proj_conv, w_out, ln2_g, ln2_b, ffn2_w_up, ffn2_w_down,
              ln3_g, ln3_b):
    for l in range(DEPTH):
        x = layer_norm(ALPHA * x + 0.5 * swiglu(x, ffn1_w_up[l], ffn1_w_down[l]), ln1_g[l], ln1_b[l])
        mix = hybrid_mixer(x, w_in[l], b_gate[l], conv_w[l], conv_b[l],
                           w_proj_attn[l], w_proj_conv[l], w_out[l])
        x = layer_norm(ALPHA * x + mix, ln2_g[l], ln2_b[l])
        x = layer_norm(ALPHA * x + 0.5 * swiglu(x, ffn2_w_up[l], ffn2_w_down[l]), ln3_g[l], ln3_b[l])
    return x
```

```python
import os
from contextlib import ExitStack
from functools import partial

import numpy as np
import concourse.bass as bass
import concourse.mybir as mybir
from concourse.bass_utils import run_bass_kernel_spmd

F32 = mybir.dt.float32
BF16 = mybir.dt.bfloat16
ALU = mybir.AluOpType
AF = mybir.ActivationFunctionType
AX = mybir.AxisListType

D = 1024
S = 2048
T = 512
NT = S // T
DFF = 2816
NJ = DFF // 128
HD = 128
NH = 8
ALPHA = 2.0 ** 0.25
LN_EPS = 1e-5
EPS_S = LN_EPS / (ALPHA * ALPHA)
C_FFN = 0.5 / ALPHA
C_MIX = 1.0 / ALPHA
SCALE = HD ** -0.5
NSLOT = int(os.environ.get("K_NSLOT", "6"))
SLOTF = 3072
NSCR = 8
NEG = -1.0e30

COMPUTE = ("pe", "act", "dve", "pool")


class Prog:
    def __init__(self, nc):
        self.nc = nc
        self.streams = {e: [] for e in ("pe", "act", "dve", "pool", "sp")}
        self.semh = {}
        self.cnt = {}
        self.seen = {e: {} for e in self.streams}
        self.lastw = {}
        self.readers = {}
        self.pos = {e: 0 for e in self.streams}
        self.pos_of = {}

    def add_sem(self, key, handle):
        self.semh[key] = handle
        self.cnt[key] = 0

    def _collect(self, eng, reads, writes):
        deps = {}

        def add(d, raw):
            if d is None:
                return
            k, v = d
            if k == eng and eng == "pe":
                return
            if v > deps.get(k, 0):
                deps[k] = v

        for t in reads:
            add(self.lastw.get(t), True)
        for t in writes:
            add(self.lastw.get(t), False)
            for k, v in self.readers.get(t, {}).items():
                add((k, v), False)
        need = []
        for k, v in deps.items():
            if v > self.seen[eng].get(k, 0):
                self.seen[eng][k] = v
                need.append((k, v))
        return need

    def op(self, eng, fn, reads=(), writes=()):
        psr = [t for t in reads if isinstance(t, tuple) and t[0] == "ps"]
        if psr:
            reads = [t for t in reads if t not in psr]
            writes = list(writes) + psr
            raw_ps = psr
        else:
            raw_ps = ()
        need = self._collect(eng, list(reads) + list(raw_ps), writes)
        self.cnt[eng] += 1
        v = self.cnt[eng]
        self.pos[eng] += 1
        self.pos_of[(eng, v)] = self.pos[eng]
        self.streams[eng].append((need, fn, (eng, 1)))
        for t in reads:
            self.readers.setdefault(t, {})[eng] = v
        for t in writes:
            self.lastw[t] = (eng, v)
            self.readers[t] = {}

    def dma(self, queue, semkey, fn, reads=(), writes=()):
        need = self._collect(queue, reads, writes)
        self.cnt[semkey] += 16
        v = self.cnt[semkey]
        self.pos[queue] += 1
        self.streams[queue].append((need, fn, (semkey, 16)))
        for t in reads:
            self.readers.setdefault(t, {})[semkey] = v
        for t in writes:
            self.lastw[t] = (semkey, v)
            self.readers[t] = {}

    def final_wait(self, eng, semkeys):
        need = [(k, self.cnt[k]) for k in semkeys if self.cnt[k] > 0]
        self.streams[eng].append((need, None, None))

    def emit(self, block):
        P = self

        def run(engobj, name):
            for need, fn, inc in P.streams[name]:
                for k, v in need:
                    engobj.wait_ge(P.semh[k], v)
                if fn is None:
                    continue
                ins = fn(engobj)
                ins.then_inc(P.semh[inc[0]], inc[1])

        @block.tensor
        def _(e):
            run(e, "pe")

        @block.scalar
        def _(e):
            run(e, "act")

        @block.vector
        def _(e):
            run(e, "dve")

        @block.gpsimd
        def _(e):
            run(e, "pool")

        @block.sync
        def _(e):
            run(e, "sp")


class RR:
    def __init__(self, items):
        self.items = list(items)
        self.i = 0

    def next(self):
        x = self.items[self.i % len(self.items)]
        self.i += 1
        return x


def panel_schedule():
    sch = []
    for j in range(NJ):
        sch.append(("UP1", j, 8, 256))
    for n in range(8):
        sch.append(("DN1", n, NJ, 128))
    for i in range(4):
        sch.append(("KP", i, 8, 256))
    for i in range(4):
        sch.append(("QP", i, 8, 256))
    for i in range(4):
        sch.append(("VP", i, 8, 256))
    for c in range(8):
        sch.append(("CP", c, 8, 384))
    for c in range(8):
        sch.append(("GA", c, 8, 256))
        sch.append(("PP", c, 8, 256))
    for i in range(4):
        sch.append(("WO", i, 8, 256))
    for j in range(NJ):
        sch.append(("UP2", j, 8, 256))
    for n in range(8):
        sch.append(("DN2", n, NJ, 128))
    return sch


SCH = panel_schedule()
SCH_F = [kc * nc_ for (_, _, kc, nc_) in SCH]
SCH_OFF = np.concatenate([[0], np.cumsum([128 * f for f in SCH_F])]).astype(np.int64)
WTOT = int(SCH_OFF[-1])
NV = 96
NCST = 128 + 128 + 512


def _pan(W, cols):
    sub = W[:, cols]
    K, n = sub.shape
    kc = K // 128
    return np.ascontiguousarray(sub.reshape(kc, 128, n).transpose(1, 0, 2)).reshape(128, kc * n)


def pack_weights(inp):
    out = np.empty((WTOT,), np.float32)
    w_in = inp["w_in"][0]
    ar = np.arange
    for p, (kind, idx, kc, ncol) in enumerate(SCH):
        if kind in ("UP1", "UP2"):
            W = inp["ffn1_w_up" if kind == "UP1" else "ffn2_w_up"][0]
            cols = np.concatenate([ar(idx * 128, idx * 128 + 128), ar(DFF + idx * 128, DFF + idx * 128 + 128)])
        elif kind in ("DN1", "DN2"):
            W = inp["ffn1_w_down" if kind == "DN1" else "ffn2_w_down"][0]
            cols = ar(idx * 128, idx * 128 + 128)
        elif kind == "QP":
            W, cols = w_in, ar(idx * 256, idx * 256 + 256)
        elif kind == "KP":
            W, cols = w_in, ar(1024 + idx * 256, 1024 + idx * 256 + 256)
        elif kind == "VP":
            W, cols = w_in, ar(2048 + idx * 256, 2048 + idx * 256 + 256)
        elif kind == "CP":
            W = w_in
            cols = np.concatenate([ar(3072 + idx * 128, 3072 + idx * 128 + 128),
                                   ar(4096 + idx * 128, 4096 + idx * 128 + 128),
                                   ar(5120 + idx * 128, 5120 + idx * 128 + 128)])
        elif kind == "GA":
            W = w_in
            cols = np.concatenate([ar(6144 + idx * 128, 6144 + idx * 128 + 128),
                                   ar(7168 + idx * 128, 7168 + idx * 128 + 128)])
        elif kind == "PP":
            W = np.concatenate([inp["w_proj_attn"][0][:, idx * 128:(idx + 1) * 128],
                                inp["w_proj_conv"][0][:, idx * 128:(idx + 1) * 128]], axis=1)
            cols = ar(0, 256)
        elif kind == "WO":
            W, cols = inp["w_out"][0], ar(idx * 256, idx * 256 + 256)
        else:
            raise ValueError(kind)
        out[SCH_OFF[p]:SCH_OFF[p + 1]] = _pan(W, cols).reshape(-1)
    return out


def pack_vec(inp):
    def col(v):
        return np.asarray(v, np.float32).reshape(-1, 128).T
    parts = [col(inp["ln1_g"][0]), col(inp["ln1_b"][0]), col(inp["ln2_g"][0]), col(inp["ln2_b"][0]),
             col(inp["ln3_g"][0]), col(inp["ln3_b"][0]), col(inp["b_gate"][0]),
             col(inp["conv_w"][0][0]), col(inp["conv_w"][0][1]), col(inp["conv_w"][0][2]),
             col(inp["conv_b"][0])]
    v = np.concatenate(parts, axis=1)
    assert v.shape == (128, NV)
    return np.ascontiguousarray(v)


V_LN = [(0, 8), (16, 24), (32, 40)]
V_BG = 48
V_CW = (64, 72, 80)
V_CB = 88


def pack_cst():
    c = np.zeros((128, NCST), np.float32)
    c[:, 0:128] = np.eye(128, dtype=np.float32)
    i = np.arange(128)[:, None]
    j = np.arange(128)[None, :]
    c[:, 128:256] = (i <= j).astype(np.float32)
    pb = np.zeros((2, 4, 8, 8), np.float32)
    for ti in range(2):
        for sub in range(4):
            blk = 2 * (ti + 2) + sub // 2
            pb[ti, sub, :, blk:] = NEG
    c[:, 256:768] = pb.reshape(1, 512)
    return c


def build_nc(ntiles=NT, stop=None):
    nc = bass.Bass("TRN2", target_bir_lowering=False)
    xT = nc.dram_tensor("xT", [NT, 128, 8 * T], F32, kind="ExternalInput").ap()
    wpk = nc.dram_tensor("wpk", [WTOT], F32, kind="ExternalInput").ap()
    vecd = nc.dram_tensor("vec", [128, NV], F32, kind="ExternalInput").ap()
    cstd = nc.dram_tensor("cst", [128, NCST], F32, kind="ExternalInput").ap()
    outT = nc.dram_tensor("outT", [D, S], F32, kind="ExternalOutput").ap()
    oTv = outT.rearrange("(c p) s -> p c s", p=128)

    with ExitStack() as es:
        def sb(name, shape, dt):
            return es.enter_context(nc.sbuf_tensor(name, shape, dt))

        P = Prog(nc)
        semnames = ["pe", "act", "dve", "pool", "xbl", "xfl", "vecl", "cstl"] + \
                   ["slot%d" % i for i in range(NSLOT)] + ["st%d" % c for c in range(8)]
        for k in semnames:
            P.add_sem(k, es.enter_context(nc.semaphore(k)))

        KT = sb("KT", [128, NH, S], BF16)
        VV = sb("VV", [128, S // 128, D], BF16)
        xf = sb("xf", [128, 8, T], F32)
        xb = sb("xb", [128, 8, T], BF16)
        hT = sb("hT", [128, 24, T], BF16)
        ring = [sb("ring%d" % i, [128, SLOTF], BF16) for i in range(NSLOT)]
        vec = sb("vecs", [128, NV], F32)
        cstf = sb("cstf", [128, NCST], F32)
        identb = sb("identb", [128, 128], BF16)
        trib = sb("trib", [128, 128], BF16)
        onesf = sb("onesf", [128, 128], F32)
        onesb = sb("onesb", [128, 128], BF16)
        epst = sb("epst", [128, 2], F32)
        dummy = sb("lndummy", [128, 1], F32)
        warm = sb("warm", [128, T], BF16)
        scr = [sb("scr%d" % i, [128, T], F32) for i in range(NSCR)]
        acc1 = sb("acc1", [128, T], F32)
        acc2 = sb("acc2", [128, T], F32)
        rstd = sb("rstd", [128, T], F32)
        ub = [sb("ub%d" % i, [128, T + 2], F32) for i in range(2)]
        cvs = [sb("cvs%d" % i, [128, T], F32) for i in range(4)]
        uhalo = sb("uhalo", [128, 8, 2], F32)
        PT = [sb("PT%d" % i, [128, T], BF16) for i in range(5)]
        planes = sb("planes", [128, 7, T], BF16)
        Gm = sb("Gm", [128, 256], F32)
        top8 = sb("top8", [128, 32, 8], F32)
        selb = sb("selb", [128, 32, 8], BF16)
        ksum = sb("ksum", [128, NH, 8], F32)
        kmT = sb("kmT", [128, NH, 8], BF16)
        pbk = [es.enter_context(nc.psum_tensor("pb%d" % i, [128, T], F32)) for i in range(8)]
        print("sbuf bytes remaining:", nc.sbuf_bytes_remaining)
        block = es.enter_context(nc.Block())

        ps_all = RR(range(8))
        ps_S = RR([0, 1])
        ps_O = RR([2, 3])
        ps_D = RR([4, 5])
        ps_P = RR([6, 7])
        scr_rr = RR(range(NSCR))
        pt_rr = RR(range(5))
        ub_rr = RR(range(2))
        cv_rr = RR(range(4))

        def PS(b):
            return ("ps", b)

        def SC(i):
            return ("scr", i)

        XB = [("xb", c) for c in range(8)]

        def HT(c):
            return ("hT", c)

        P.dma("sp", "vecl", lambda e: e.dma_start(out=vec[:], in_=vecd), writes=["vec"])
        P.dma("sp", "cstl", lambda e: e.dma_start(out=cstf[:], in_=cstd), writes=["cstf"])
        P.op("pool", lambda e: e.memset(onesf[:], 1.0), writes=["onesf"])
        P.op("pool", lambda e: e.memset(onesb[:], 1.0), writes=["onesb"])
        P.op("pool", lambda e: e.memset(warm[:], 1.0), writes=["warm"])
        P.op("pool", lambda e: e.memset(epst[:], EPS_S), writes=["epst"])
        P.op("pool", lambda e: e.memset(uhalo[:], 0.0), writes=[("uhalo", c) for c in range(8)])
        P.op("pool", lambda e: e.memset(kmT[:], 0.0), writes=[("kmT", h) for h in range(NH)])
        P.op("pool", lambda e: e.memset(ksum[:], 0.0), writes=[("ksum", h) for h in range(NH)])
        P.op("dve", lambda e: e.tensor_copy(out=identb[:], in_=cstf[:, 0:128]), reads=["cstf"], writes=["identb"])
        P.op("dve", lambda e: e.tensor_copy(out=trib[:], in_=cstf[:, 128:256]), reads=["cstf"], writes=["trib"])

        total_panels = ntiles * len(SCH)
        st = {"issued": 0, "used": 0}

        def issue_panel(gi):
            p = gi % len(SCH)
            slot = gi % NSLOT
            F = SCH_F[p]
            src = wpk[int(SCH_OFF[p]):int(SCH_OFF[p + 1])].rearrange("(p f) -> p f", p=128)

            def fn(e, slot=slot, F=F, src=src):
                return e.dma_start(out=ring[slot][:, 0:F], in_=src)
            gate = [("xb", 0)] if 1 <= gi < NSLOT else []
            P.dma("pool", "slot%d" % slot, fn, reads=gate, writes=[("w", slot)])

        def take_panel(kind, idx):
            gi = st["used"]
            p = gi % len(SCH)
            assert SCH[p][0] == kind and SCH[p][1] == idx, (SCH[p], kind, idx)
            base = st.get("hold")
            if base is None:
                base = gi
            while st["issued"] < min(base + NSLOT, total_panels):
                issue_panel(st["issued"])
                st["issued"] += 1
            st["used"] += 1
            slot = gi % NSLOT
            return ring[slot], ("w", slot)

        def mm_group(bank, W, col0, ncol_stride, rhs_chunks, nk, wtok, rtoks, m=128):
            def fn(e):
                ins = None
                for k in range(nk):
                    ins = e.matmul(pbk[bank][:, :], lhsT=W[:, k * ncol_stride + col0:k * ncol_stride + col0 + m],
                                   rhs=rhs_chunks(k), start=(k == 0), stop=(k == nk - 1))
                return ins
            P.op("pe", fn, reads=[wtok] + list(rtoks), writes=[PS(bank)])

        def wide_mm(specs, rhs_k, rtok_k, nk=8):
            for k in range(nk):
                def fn(e, k=k):
                    ins = None
                    for (bank, W, wt, col0, stride) in specs:
                        ins = e.matmul(pbk[bank][:, :], lhsT=W[:, k * stride + col0:k * stride + col0 + 128],
                                       rhs=rhs_k(k), start=(k == 0), stop=(k == nk - 1))
                    return ins
                P.op("pe", fn, reads=[sp_[2] for sp_ in specs] + [rtok_k(k)], writes=[PS(sp_[0]) for sp_ in specs])

        def ln_stats(n):
            xn = ("xf", n)
            if n == 0:
                P.op("act", lambda e: e.activation(out=acc2[:], in_=xf[:, 0, :], func=AF.Square), reads=[xn], writes=["acc2"])
                P.op("dve", lambda e: e.tensor_copy(out=acc1[:], in_=xf[:, 0, :]), reads=[xn], writes=["acc1"])
            else:
                si = scr_rr.next()
                P.op("act", lambda e: e.activation(out=scr[si][:], in_=xf[:, n, :], func=AF.Square), reads=[xn], writes=[SC(si)])
                P.op("dve", lambda e: e.tensor_tensor(out=acc1[:], in0=acc1[:], in1=xf[:, n, :], op=ALU.add), reads=["acc1", xn], writes=["acc1"])
                P.op("dve", lambda e: e.tensor_tensor(out=acc2[:], in0=acc2[:], in1=scr[si][:], op=ALU.add), reads=["acc2", SC(si)], writes=["acc2"])
                if n == 7:
                    P.op("act", lambda e: e.activation(out=dummy[:], in_=epst[:, 0:1], func=AF.Ln), reads=["epst"], writes=["dummy"])

        def layer_norm(li, want_bf16, after_chunk=None):
            g0, b0 = V_LN[li]
            b1 = ps_all.next()
            b2 = ps_all.next()
            P.op("pe", lambda e: e.matmul(pbk[b1][:, :], lhsT=onesf[:], rhs=acc1[:], start=True, stop=True),
                 reads=["onesf", "acc1"], writes=[PS(b1)])
            P.op("pe", lambda e: e.matmul(pbk[b2][:, :], lhsT=onesf[:], rhs=acc2[:], start=True, stop=True),
                 reads=["onesf", "acc2"], writes=[PS(b2)])
            NWARM = int(os.environ.get("K_NWARM", "24")) if li == 0 else int(os.environ.get("K_NWARM2", "17"))
            if li < 2 and NWARM:
                bw = ps_all.next()

                def wfn(e, bw=bw):
                    ins = None
                    for i in range(NWARM):
                        ins = e.matmul(pbk[bw][:, :], lhsT=identb[:], rhs=warm[:], start=True, stop=True)
                    return ins
                P.op("pe", wfn, reads=["identb", "warm"], writes=[PS(bw)])
            mean = acc1
            var = acc2
            si = scr_rr.next()
            P.op("dve", lambda e: e.tensor_scalar(out=mean[:], in0=pbk[b1][:, :], scalar1=1.0 / D, scalar2=None, op0=ALU.mult),
                 reads=[PS(b1)], writes=["acc1"])
            P.op("dve", lambda e: e.tensor_tensor(out=scr[si][:], in0=mean[:], in1=mean[:], op=ALU.mult), reads=["acc1"], writes=[SC(si)])
            P.op("dve", lambda e: e.scalar_tensor_tensor(out=var[:], in0=pbk[b2][:, :], scalar=1.0 / D, in1=scr[si][:],
                                                         op0=ALU.mult, op1=ALU.subtract),
                 reads=[PS(b2), SC(si)], writes=["acc2"])
            P.op("act", lambda e: e.activation(out=rstd[:], in_=var[:], func=AF.Ln, bias=epst[:, 0:1], scale=1.0),
                 reads=["acc2", "epst"], writes=["rstd"])
            P.op("act", lambda e: e.activation(out=rstd[:], in_=rstd[:], func=AF.Exp, scale=-0.5),
                 reads=["rstd"], writes=["rstd"])
            tsc = {}

            def t_op(c):
                s1 = scr_rr.next()
                tsc[c] = s1
                P.op("dve", lambda e, c=c, s1=s1: e.tensor_tensor(out=scr[s1][:], in0=xf[:, c, :], in1=mean[:], op=ALU.subtract),
                     reads=[("xf", c), "acc1"], writes=[SC(s1)])

            def affine(c, bf):
                dst = xb[:, c, :] if bf else xf[:, c, :]
                P.op("act", lambda e, c=c, dst=dst: e.activation(out=dst, in_=xf[:, c, :], func=AF.Identity,
                                                                 bias=vec[:, b0 + c:b0 + c + 1], scale=vec[:, g0 + c:g0 + c + 1]),
                     reads=[("xf", c), "vec"], writes=[XB[c] if bf else ("xf", c)])

            AHEAD = 3
            for c in range(AHEAD):
                t_op(c)
            for c in range(8):
                s1 = tsc[c]
                P.op("dve", lambda e, c=c, s1=s1: e.tensor_tensor(out=xf[:, c, :], in0=scr[s1][:], in1=rstd[:], op=ALU.mult),
                     reads=[SC(s1), "rstd"], writes=[("xf", c)])
                if c + AHEAD < 8:
                    t_op(c + AHEAD)
                if want_bf16:
                    affine(c, True)
                else:
                    affine(c, False)
                    if after_chunk is not None:
                        after_chunk(c)
            if want_bf16:
                for c in range(8):
                    affine(c, False)
                    if after_chunk is not None:
                        after_chunk(c)

        def ffn(which, after_up=None, wide=False):
            up, dn = ("UP1", "DN1") if which == 1 else ("UP2", "DN2")
            NW = 3 if wide else 0
            wbanks = {}
            if wide:
                specs = []
                st["hold"] = st["used"]
                for j in range(NW):
                    W, wt = take_panel(up, j)
                    bg = ps_all.next()
                    bu = ps_all.next()
                    wbanks[j] = (bg, bu)
                    specs.append((bg, W, wt, 0, 256))
                    specs.append((bu, W, wt, 128, 256))
                wide_mm(specs, lambda k: xb[:, k, :], lambda k: XB[k])
                st["hold"] = None
            for j in range(NJ):
                if j < NW:
                    bg, bu = wbanks[j]
                else:
                    W, wt = take_panel(up, j)
                    bg = ps_all.next()
                    bu = ps_all.next()
                    mm_group(bg, W, 0, 256, lambda k: xb[:, k, :], 8, wt, XB)
                    mm_group(bu, W, 128, 256, lambda k: xb[:, k, :], 8, wt, XB)
                si = scr_rr.next()
                P.op("act", lambda e, bg=bg, si=si: e.activation(out=scr[si][:], in_=pbk[bg][:, :], func=AF.Silu),
                     reads=[PS(bg)], writes=[SC(si)])
                P.op("dve", lambda e, bu=bu, si=si, j=j: e.tensor_tensor(out=hT[:, j, :], in0=scr[si][:], in1=pbk[bu][:, :], op=ALU.mult),
                     reads=[SC(si), PS(bu)], writes=[HT(j)])
                if j == 3 and not st.get("xf0_loaded"):
                    st["xf0_loaded"] = True
                    load_xf(0, gate=[HT(3)])
            if after_up is not None:
                after_up()
            for n in range(8):
                W, wt = take_panel(dn, n)
                by = ps_all.next()
                mm_group(by, W, 0, 128, lambda k: hT[:, k, :], NJ, wt, [HT(k) for k in range(NJ)])
                P.op("dve", lambda e, by=by, n=n: e.scalar_tensor_tensor(out=xf[:, n, :], in0=pbk[by][:, :], scalar=C_FFN, in1=xf[:, n, :],
                                                                        op0=ALU.mult, op1=ALU.add),
                     reads=[PS(by), ("xf", n)], writes=[("xf", n)])
                ln_stats(n)

        def mixer(t):
            tok0 = t * T
            kb = {}
            specs = []
            NWK = int(os.environ.get("K_NWK", "3"))
            if NWK:
                st["hold"] = st["used"]
            for i in range(NWK):
                W, wt = take_panel("KP", i)
                for hh in range(2):
                    b = ps_all.next()
                    kb[2 * i + hh] = b
                    specs.append((b, W, wt, hh * 128, 256))
            if NWK:
                wide_mm(specs, lambda k: xb[:, k, :], lambda k: XB[k])
                st["hold"] = None
            for i in range(4):
                if i >= NWK:
                    W, wt = take_panel("KP", i)
                for hh in range(2):
                    h = 2 * i + hh
                    if i < NWK:
                        b = kb[h]
                    else:
                        b = ps_all.next()
                        mm_group(b, W, hh * 128, 256, lambda k: xb[:, k, :], 8, wt, XB)
                    P.op("act", lambda e, b=b, h=h: e.copy(out=KT[:, h, tok0:tok0 + T], in_=pbk[b][:, :]),
                         reads=[PS(b)], writes=[("KT", h, t)])
                    P.op("dve", lambda e, b=b, h=h: e.tensor_reduce(out=ksum[:, h, 2 * t:2 * t + 2],
                                                                   in_=pbk[b][:, :].rearrange("p (a b) -> p a b", a=2),
                                                                   axis=AX.X, op=ALU.add),
                         reads=[PS(b)], writes=[("ksum", h)])
                    P.op("act", lambda e, h=h: e.mul(out=kmT[:, h, 2 * t:2 * t + 2], in_=ksum[:, h, 2 * t:2 * t + 2], mul=1.0 / 256.0),
                         reads=[("ksum", h)], writes=[("kmT", h)])
            for i in range(4):
                W, wt = take_panel("QP", i)
                for hh in range(2):
                    h = 2 * i + hh
                    b = ps_all.next()
                    mm_group(b, W, hh * 128, 256, lambda k: xb[:, k, :], 8, wt, XB)
                    P.op("dve", lambda e, b=b, h=h: e.tensor_copy(out=hT[:, h, :], in_=pbk[b][:, :]), reads=[PS(b)], writes=[HT(h)])
            need_sel = t >= 2
            if need_sel:
                def gfn(e):
                    ins = None
                    for sub in range(4):
                        for h in range(NH):
                            j = sub * 8 + h
                            ins = e.matmul(pbk[7][:, j * 8:(j + 1) * 8], lhsT=hT[:, h, sub * 128:(sub + 1) * 128],
                                           rhs=kmT[:, h, :], start=True, stop=True)
                    return ins
                P.op("pe", gfn, reads=[HT(h) for h in range(NH)] + [("kmT", h) for h in range(NH)], writes=[PS(7)])
                P.op("dve", lambda e: e.tensor_tensor(out=Gm[:], in0=pbk[7][:, 0:256], in1=cstf[:, 256 + (t - 2) * 256:256 + (t - 1) * 256], op=ALU.add),
                     reads=[PS(7), "cstf"], writes=["Gm"])
                for j in range(32):
                    P.op("dve", lambda e, j=j: e.max(out=top8[:, j, :], in_=Gm[:, j * 8:(j + 1) * 8]), reads=["Gm"], writes=["top8"])
                P.op("dve", lambda e: e.tensor_tensor(out=selb[:], in0=Gm[:].rearrange("p (a b) -> p a b", b=8),
                                                      in1=top8[:, :, 2:3].to_broadcast([128, 32, 8]), op=ALU.is_ge),
                     reads=["Gm", "top8"], writes=["selb"])
            init_planes = list(range(2 * t + 1)) if need_sel else []
            init_banks = RR([6, 7])

            def make_plane0(n):
                bp = init_banks.next()

                def pfn(e, n=n, bp=bp):
                    ins = None
                    for sub in range(4):
                        ins = e.matmul(pbk[bp][:, sub * 128:(sub + 1) * 128],
                                       lhsT=selb[:, sub * 8 + 0, n:n + 1].to_broadcast([128, 128]),
                                       rhs=identb[:], start=True, stop=True)
                    return ins
                P.op("pe", pfn, reads=["selb", "identb"], writes=[PS(bp)])
                P.op("dve", lambda e, n=n, bp=bp: e.tensor_copy(out=planes[:, n, :], in_=pbk[bp][:, :]), reads=[PS(bp)], writes=[("pl", n)])
            for i in range(4):
                W, wt = take_panel("VP", i)
                for sp in range(2):
                    b = ps_all.next()

                    def fn(e, b=b, sp=sp, W=W):
                        ins = None
                        for half in range(2):
                            sub = 2 * sp + half
                            for k in range(8):
                                ins = e.matmul(pbk[b][:, half * 256:(half + 1) * 256], lhsT=xb[:, k, sub * 128:(sub + 1) * 128],
                                               rhs=W[:, k * 256:(k + 1) * 256], start=(k == 0), stop=(k == 7))
                        return ins
                    P.op("pe", fn, reads=[wt] + XB, writes=[PS(b)])
                    eng = "act" if sp == 0 else "dve"

                    def ev(e, b=b, sp=sp, i=i, eng=eng):
                        o = VV[:, 4 * t + 2 * sp:4 * t + 2 * sp + 2, i * 256:(i + 1) * 256]
                        src = pbk[b][:, :].rearrange("p (a b) -> p a b", a=2)
                        if eng == "act":
                            return e.copy(out=o, in_=src)
                        return e.tensor_copy(out=o, in_=src)
                    P.op(eng, ev, reads=[PS(b)], writes=[("VV", t, i, sp)])
                    vg = 2 * i + sp
                    if vg >= 4:
                        for _ in range(2):
                            if init_planes:
                                make_plane0(init_planes.pop(0))
            if stop == "kqv":
                return 0
            while init_planes:
                make_plane0(init_planes.pop(0))
            if stop == "kqv":
                return 0
            nkc = 4 * (t + 1)
            if need_sel:
                conv_banks = RR([7])
                plane_banks = RR([6])
            else:
                conv_banks = RR([6, 7])
                plane_banks = None

            def conv_gen():
                for c in range(8):
                    W, wt = take_panel("CP", c)
                    bh = conv_banks.next()
                    mm_group(bh, W, 0, 384, lambda k: xb[:, k, :], 8, wt, XB)
                    sh = cv_rr.next()
                    P.op("dve", lambda e, bh=bh, sh=sh: e.tensor_copy(out=cvs[sh][:], in_=pbk[bh][:, :]), reads=[PS(bh)], writes=[("cvs", sh)])
                    yield
                    bc = conv_banks.next()
                    mm_group(bc, W, 256, 384, lambda k: xb[:, k, :], 8, wt, XB)
                    ui = ub_rr.next()
                    U = ("ub", ui)
                    P.op("pool", lambda e, ui=ui, c=c: e.tensor_copy(out=ub[ui][:, 0:2], in_=uhalo[:, c, :]), reads=[("uhalo", c)], writes=[U])
                    P.op("dve", lambda e, ui=ui, sh=sh, bc=bc: e.tensor_tensor(out=ub[ui][:, 2:T + 2], in0=cvs[sh][:], in1=pbk[bc][:, :], op=ALU.mult),
                         reads=[("cvs", sh), PS(bc)], writes=[U])
                    P.op("pool", lambda e, ui=ui, c=c: e.tensor_copy(out=uhalo[:, c, :], in_=ub[ui][:, T:T + 2]), reads=[U], writes=[("uhalo", c)])
                    yield
                    bb = conv_banks.next()
                    mm_group(bb, W, 128, 384, lambda k: xb[:, k, :], 8, wt, XB)
                    sa = cv_rr.next()
                    A = ("cvs", sa)
                    P.op("dve", lambda e, ui=ui, sa=sa, c=c: e.tensor_scalar(out=cvs[sa][:], in0=ub[ui][:, 2:T + 2],
                                                                             scalar1=vec[:, V_CW[2] + c:V_CW[2] + c + 1], scalar2=vec[:, V_CB + c:V_CB + c + 1],
                                                                             op0=ALU.mult, op1=ALU.add),
                         reads=[U, "vec"], writes=[A])
                    yield
                    P.op("dve", lambda e, ui=ui, sa=sa, c=c: e.scalar_tensor_tensor(out=cvs[sa][:], in0=ub[ui][:, 1:T + 1], scalar=vec[:, V_CW[1] + c:V_CW[1] + c + 1],
                                                                                    in1=cvs[sa][:], op0=ALU.mult, op1=ALU.add),
                         reads=[U, A, "vec"], writes=[A])
                    P.op("dve", lambda e, ui=ui, sa=sa, c=c: e.scalar_tensor_tensor(out=cvs[sa][:], in0=ub[ui][:, 0:T], scalar=vec[:, V_CW[0] + c:V_CW[0] + c + 1],
                                                                                    in1=cvs[sa][:], op0=ALU.mult, op1=ALU.add),
                         reads=[U, A, "vec"], writes=[A])
                    yield
                    P.op("dve", lambda e, sa=sa, bb=bb, c=c: e.tensor_tensor(out=hT[:, 16 + c, :], in0=cvs[sa][:], in1=pbk[bb][:, :], op=ALU.mult),
                         reads=[A, PS(bb)], writes=[HT(16 + c)])
                    yield

            def make_plane(h, n):
                bp = plane_banks.next()

                def pfn(e, h=h, n=n, bp=bp):
                    ins = None
                    for sub in range(4):
                        ins = e.matmul(pbk[bp][:, sub * 128:(sub + 1) * 128],
                                       lhsT=selb[:, sub * 8 + h, n:n + 1].to_broadcast([128, 128]),
                                       rhs=identb[:], start=True, stop=True)
                    return ins
                P.op("pe", pfn, reads=["selb", "identb"], writes=[PS(bp)])
                P.op("dve", lambda e, n=n, bp=bp: e.tensor_copy(out=planes[:, n, :], in_=pbk[bp][:, :]), reads=[PS(bp)], writes=[("pl", n)])

            hb = {}

            def banks_of(h):
                if h not in hb:
                    hb[h] = (ps_O.next(), ps_D.next())
                return hb[h]

            def emit_S(h, kc):
                tt = kc // 4
                q0 = (kc - 4 * t) * 128 if tt == t else 0
                bs = ps_S.next()
                pi = pt_rr.next()
                P.op("pe", lambda e: e.matmul(pbk[bs][:, q0:T], lhsT=KT[:, h, kc * 128:(kc + 1) * 128],
                                              rhs=hT[:, h, q0:T], start=True, stop=True),
                     reads=[("KT", h, tt), HT(h)], writes=[PS(bs)])
                P.op("act", lambda e: e.activation(out=PT[pi][:, q0:T], in_=pbk[bs][:, q0:T], func=AF.Exp, scale=SCALE),
                     reads=[PS(bs)], writes=[("PT", pi)])
                n = kc // 2
                if tt == t:
                    P.op("dve", lambda e: e.tensor_tensor(out=PT[pi][:, q0:q0 + 128], in0=PT[pi][:, q0:q0 + 128], in1=trib[:], op=ALU.mult),
                         reads=[("PT", pi), "trib"], writes=[("PT", pi)])
                    if need_sel and n == 2 * t:
                        P.op("dve", lambda e: e.tensor_tensor(out=PT[pi][:, 256:T], in0=PT[pi][:, 256:T], in1=planes[:, n, 256:T], op=ALU.mult),
                             reads=[("PT", pi), ("pl", n)], writes=[("PT", pi)])
                elif need_sel:
                    P.op("dve", lambda e: e.tensor_tensor(out=PT[pi][:, :], in0=PT[pi][:, :], in1=planes[:, n, :], op=ALU.mult),
                         reads=[("PT", pi), ("pl", n)], writes=[("PT", pi)])
                if need_sel and h + 1 < NH and n <= 2 * t and kc == 2 * n + 1:
                    make_plane(h + 1, n)
                return pi, q0, tt

            def emit_PV(h, kc, pi, q0, tt):
                bo, bd = banks_of(h)
                first = kc == 0
                last = kc == nkc - 1
                sub = kc % 4

                def fn(e):
                    e.matmul(pbk[bo][:, q0:T], lhsT=VV[:, kc, h * 128:(h + 1) * 128], rhs=PT[pi][:, q0:T],
                             start=first, stop=last, skip_group_check=True)
                    return e.matmul(pbk[bd][:, q0:T], lhsT=onesb[:], rhs=PT[pi][:, q0:T],
                                    start=first, stop=last, skip_group_check=True)
                P.op("pe", fn, reads=[("VV", tt, h // 2, sub // 2), ("PT", pi), "onesb"], writes=[PS(bo), PS(bd)])
                if last:
                    si = scr_rr.next()
                    P.op("act", lambda e: e.activation(out=scr[si][:], in_=pbk[bd][:, :], func=AF.Ln), reads=[PS(bd)], writes=[SC(si)])
                    P.op("act", lambda e: e.activation(out=scr[si][:], in_=scr[si][:], func=AF.Exp, scale=-1.0), reads=[SC(si)], writes=[SC(si)])
                    P.op("dve", lambda e: e.tensor_tensor(out=hT[:, 8 + h, :], in0=pbk[bo][:, :], in1=scr[si][:], op=ALU.mult),
                         reads=[PS(bo), SC(si)], writes=[HT(8 + h)])

            jobs = [(h, kc) for h in range(NH) for kc in range(nkc)]
            conv_it = conv_gen()
            NPIECE = 8 * 5
            every = max(1, len(jobs) // NPIECE)
            LOOK = 4
            sres = {}
            for step in range(len(jobs) + LOOK):
                if step < len(jobs):
                    sres[step] = emit_S(*jobs[step])
                if step >= LOOK:
                    j = step - LOOK
                    emit_PV(*jobs[j], *sres[j])
                if step % every == 0:
                    next(conv_it, None)
            for _ in conv_it:
                pass
            if stop == "attn":
                return 8
            if stop == "conv":
                return 16
            for c in range(8):
                Wg, wtg = take_panel("GA", c)
                bga = ps_all.next()
                bgc = ps_all.next()
                mm_group(bga, Wg, 0, 256, lambda k: xb[:, k, :], 8, wtg, XB)
                mm_group(bgc, Wg, 128, 256, lambda k: xb[:, k, :], 8, wtg, XB)
                Wp, wtp = take_panel("PP", c)
                bya = ps_all.next()
                byc = ps_all.next()
                mm_group(bya, Wp, 0, 256, lambda k: hT[:, 8 + k, :], 8, wtp, [HT(8 + k) for k in range(8)])
                mm_group(byc, Wp, 128, 256, lambda k: hT[:, 16 + k, :], 8, wtp, [HT(16 + k) for k in range(8)])
                s1 = scr_rr.next()
                s2 = scr_rr.next()
                P.op("act", lambda e, bga=bga, s1=s1, c=c: e.activation(out=scr[s1][:], in_=pbk[bga][:, :], func=AF.Sigmoid,
                                                                        bias=vec[:, V_BG + c:V_BG + c + 1], scale=1.0),
                     reads=[PS(bga), "vec"], writes=[SC(s1)])
                P.op("act", lambda e, bgc=bgc, s2=s2, c=c: e.activation(out=scr[s2][:], in_=pbk[bgc][:, :], func=AF.Sigmoid,
                                                                        bias=vec[:, V_BG + 8 + c:V_BG + 8 + c + 1], scale=1.0),
                     reads=[PS(bgc), "vec"], writes=[SC(s2)])
                P.op("dve", lambda e, bya=bya, s1=s1: e.tensor_tensor(out=scr[s1][:], in0=scr[s1][:], in1=pbk[bya][:, :], op=ALU.mult),
                     reads=[SC(s1), PS(bya)], writes=[SC(s1)])
                P.op("dve", lambda e, byc=byc, s2=s2: e.tensor_tensor(out=scr[s2][:], in0=scr[s2][:], in1=pbk[byc][:, :], op=ALU.mult),
                     reads=[SC(s2), PS(byc)], writes=[SC(s2)])
                P.op("dve", lambda e, s1=s1, s2=s2, c=c: e.tensor_tensor(out=hT[:, c, :], in0=scr[s1][:], in1=scr[s2][:], op=ALU.add),
                     reads=[SC(s1), SC(s2)], writes=[HT(c)])
            if stop == "merge":
                return 0
            for i in range(4):
                W, wt = take_panel("WO", i)
                for hh in range(2):
                    n = 2 * i + hh
                    b = ps_all.next()
                    mm_group(b, W, hh * 128, 256, lambda k: hT[:, k, :], 8, wt, [HT(k) for k in range(8)])
                    P.op("dve", lambda e, b=b, n=n: e.scalar_tensor_tensor(out=xf[:, n, :], in0=pbk[b][:, :], scalar=C_MIX, in1=xf[:, n, :],
                                                                          op0=ALU.mult, op1=ALU.add),
                         reads=[PS(b), ("xf", n)], writes=[("xf", n)])
                    ln_stats(n)

        def load_xb(t):
            P.dma("pool", "xbl", lambda e: e.dma_start(out=xb[:].rearrange("p a b -> p (a b)"), in_=xT[t]), writes=XB)

        def load_xf(t, gate=()):
            P.dma("sp", "xfl", lambda e: e.dma_start(out=xf[:].rearrange("p a b -> p (a b)"), in_=xT[t]), reads=list(gate),
                  writes=[("xf", c) for c in range(8)])

        load_xb(0)

        def dump(t, from_hT=None):
            for c in range(8):
                if from_hT is not None:
                    P.op("dve", lambda e, c=c: e.tensor_copy(out=xf[:, c, :], in_=hT[:, from_hT + c, :]),
                         reads=[HT(from_hT + c)], writes=[("xf", c)])
                P.dma("sp", "st%d" % c, lambda e, c=c: e.dma_start(out=oTv[:, c, t * T:(t + 1) * T], in_=xf[:, c, :]), reads=[("xf", c)])

        for t in range(ntiles):
            last_t = t + 1 == ntiles
            ffn(1)
            if stop == "ffn1" and last_t:
                dump(t)
                break
            layer_norm(0, True)
            if stop == "ln1" and last_t:
                dump(t)
                break
            r = mixer(t)
            if r is not None:
                if last_t:
                    dump(t, from_hT=r)
                    break
                else:
                    raise ValueError("mixer stop only with single tile")
            if stop == "wout" and last_t:
                dump(t)
                break
            layer_norm(1, True)
            if stop == "ln2" and last_t:
                dump(t)
                break
            ffn(2, after_up=(partial(load_xb, t + 1) if t + 1 < ntiles else None), wide=True)

            def store(c, t=t):
                P.dma("sp", "st%d" % c, lambda e: e.dma_start(out=oTv[:, c, t * T:(t + 1) * T], in_=xf[:, c, :]), reads=[("xf", c)])
            layer_norm(2, False, after_chunk=store)
            if t + 1 < ntiles:
                load_xf(t + 1)
        P.final_wait("sp", ["st%d" % c for c in range(8)])
        P.emit(block)
        print("ops:", {k: len(v) for k, v in P.streams.items()})
    return nc


_CACHE = {}


def kernel(**inputs):
    x = np.asarray(inputs["x"], np.float32)
    B = x.shape[0]
    wp = pack_weights(inputs)
    vec = pack_vec(inputs)
    cst = pack_cst()
    if "nc" not in _CACHE:
        _CACHE["nc"] = build_nc()
    nc = _CACHE["nc"]
    in_maps = []
    for b in range(B):
        xt = np.ascontiguousarray(x[b].reshape(NT, T, 8, 128).transpose(0, 3, 2, 1)).reshape(NT, 128, 8 * T)
        in_maps.append({"xT": xt, "wpk": wp, "vec": vec, "cst": cst})
    res = run_bass_kernel_spmd(nc, in_maps, core_ids=list(range(B)))
    out = np.stack([np.ascontiguousarray(r["outT"].T) for r in res.results], axis=0)
    return out.astype(np.float32)
```

```python
import os
from contextlib import ExitStack
from functools import partial

import numpy as np
import concourse.bass as bass
import concourse.mybir as mybir
from concourse.bass_utils import run_bass_kernel_spmd

F32 = mybir.dt.float32
BF16 = mybir.dt.bfloat16
ALU = mybir.AluOpType
AF = mybir.ActivationFunctionType
AX = mybir.AxisListType

D = 1024
S = 2048
T = 512
NT = S // T
DFF = 2816
NJ = DFF // 128
HD = 128
NH = 8
ALPHA = 2.0 ** 0.25
LN_EPS = 1e-5
EPS_S = LN_EPS / (ALPHA * ALPHA)
C_FFN = 0.5 / ALPHA
C_MIX = 1.0 / ALPHA
SCALE = HD ** -0.5
NSLOT = int(os.environ.get("K_NSLOT", "6"))
SLOTF = 3072
NSCR = 8
NEG = -1.0e30

COMPUTE = ("pe", "act", "dve", "pool")


class Prog:
    def __init__(self, nc):
        self.nc = nc
        self.streams = {e: [] for e in ("pe", "act", "dve", "pool", "sp")}
        self.semh = {}
        self.cnt = {}
        self.seen = {e: {} for e in self.streams}
        self.lastw = {}
        self.readers = {}
        self.pos = {e: 0 for e in self.streams}
        self.pos_of = {}

    def add_sem(self, key, handle):
        self.semh[key] = handle
        self.cnt[key] = 0

    def _collect(self, eng, reads, writes):
        deps = {}

        def add(d, raw):
            if d is None:
                return
            k, v = d
            if k == eng and eng == "pe":
                return
            if v > deps.get(k, 0):
                deps[k] = v

        for t in reads:
            add(self.lastw.get(t), True)
        for t in writes:
            add(self.lastw.get(t), False)
            for k, v in self.readers.get(t, {}).items():
                add((k, v), False)
        need = []
        for k, v in deps.items():
            if v > self.seen[eng].get(k, 0):
                self.seen[eng][k] = v
                need.append((k, v))
        return need

    def op(self, eng, fn, reads=(), writes=()):
        psr = [t for t in reads if isinstance(t, tuple) and t[0] == "ps"]
        if psr:
            reads = [t for t in reads if t not in psr]
            writes = list(writes) + psr
            raw_ps = psr
        else:
            raw_ps = ()
        need = self._collect(eng, list(reads) + list(raw_ps), writes)
        self.cnt[eng] += 1
        v = self.cnt[eng]
        self.pos[eng] += 1
        self.pos_of[(eng, v)] = self.pos[eng]
        self.streams[eng].append((need, fn, (eng, 1)))
        for t in reads:
            self.readers.setdefault(t, {})[eng] = v
        for t in writes:
            self.lastw[t] = (eng, v)
            self.readers[t] = {}

    def dma(self, queue, semkey, fn, reads=(), writes=()):
        need = self._collect(queue, reads, writes)
        self.cnt[semkey] += 16
        v = self.cnt[semkey]
        self.pos[queue] += 1
        self.streams[queue].append((need, fn, (semkey, 16)))
        for t in reads:
            self.readers.setdefault(t, {})[semkey] = v
        for t in writes:
            self.lastw[t] = (semkey, v)
            self.readers[t] = {}

    def final_wait(self, eng, semkeys):
        need = [(k, self.cnt[k]) for k in semkeys if self.cnt[k] > 0]
        self.streams[eng].append((need, None, None))

    def emit(self, block):
        P = self

        def run(engobj, name):
            for need, fn, inc in P.streams[name]:
                for k, v in need:
                    engobj.wait_ge(P.semh[k], v)
                if fn is None:
                    continue
                ins = fn(engobj)
                ins.then_inc(P.semh[inc[0]], inc[1])

        @block.tensor
        def _(e):
            run(e, "pe")

        @block.scalar
        def _(e):
            run(e, "act")

        @block.vector
        def _(e):
            run(e, "dve")

        @block.gpsimd
        def _(e):
            run(e, "pool")

        @block.sync
        def _(e):
            run(e, "sp")


class RR:
    def __init__(self, items):
        self.items = list(items)
        self.i = 0

    def next(self):
        x = self.items[self.i % len(self.items)]
        self.i += 1
        return x


def panel_schedule():
    sch = []
    for j in range(NJ):
        sch.append(("UP1", j, 8, 256))
    for n in range(8):
        sch.append(("DN1", n, NJ, 128))
    for i in range(4):
        sch.append(("KP", i, 8, 256))
    for i in range(4):
        sch.append(("QP", i, 8, 256))
    for i in range(4):
        sch.append(("VP", i, 8, 256))
    for c in range(8):
        sch.append(("CP", c, 8, 384))
    for c in range(8):
        sch.append(("GA", c, 8, 256))
        sch.append(("PP", c, 8, 256))
    for i in range(4):
        sch.append(("WO", i, 8, 256))
    for j in range(NJ):
        sch.append(("UP2", j, 8, 256))
    for n in range(8):
        sch.append(("DN2", n, NJ, 128))
    return sch


SCH = panel_schedule()
SCH_F = [kc * nc_ for (_, _, kc, nc_) in SCH]
SCH_OFF = np.concatenate([[0], np.cumsum([128 * f for f in SCH_F])]).astype(np.int64)
WTOT = int(SCH_OFF[-1])
NV = 96
NCST = 128 + 128 + 512


def _pan(W, cols):
    sub = W[:, cols]
    K, n = sub.shape
    kc = K // 128
    return np.ascontiguousarray(sub.reshape(kc, 128, n).transpose(1, 0, 2)).reshape(128, kc * n)


def pack_weights(inp):
    out = np.empty((WTOT,), np.float32)
    w_in = inp["w_in"][0]
    ar = np.arange
    for p, (kind, idx, kc, ncol) in enumerate(SCH):
        if kind in ("UP1", "UP2"):
            W = inp["ffn1_w_up" if kind == "UP1" else "ffn2_w_up"][0]
            cols = np.concatenate([ar(idx * 128, idx * 128 + 128), ar(DFF + idx * 128, DFF + idx * 128 + 128)])
        elif kind in ("DN1", "DN2"):
            W = inp["ffn1_w_down" if kind == "DN1" else "ffn2_w_down"][0]
            cols = ar(idx * 128, idx * 128 + 128)
        elif kind == "QP":
            W, cols = w_in, ar(idx * 256, idx * 256 + 256)
        elif kind == "KP":
            W, cols = w_in, ar(1024 + idx * 256, 1024 + idx * 256 + 256)
        elif kind == "VP":
            W, cols = w_in, ar(2048 + idx * 256, 2048 + idx * 256 + 256)
        elif kind == "CP":
            W = w_in
            cols = np.concatenate([ar(3072 + idx * 128, 3072 + idx * 128 + 128),
                                   ar(4096 + idx * 128, 4096 + idx * 128 + 128),
                                   ar(5120 + idx * 128, 5120 + idx * 128 + 128)])
        elif kind == "GA":
            W = w_in
            cols = np.concatenate([ar(6144 + idx * 128, 6144 + idx * 128 + 128),
                                   ar(7168 + idx * 128, 7168 + idx * 128 + 128)])
        elif kind == "PP":
            W = np.concatenate([inp["w_proj_attn"][0][:, idx * 128:(idx + 1) * 128],
                                inp["w_proj_conv"][0][:, idx * 128:(idx + 1) * 128]], axis=1)
            cols = ar(0, 256)
        elif kind == "WO":
            W, cols = inp["w_out"][0], ar(idx * 256, idx * 256 + 256)
        else:
            raise ValueError(kind)
        out[SCH_OFF[p]:SCH_OFF[p + 1]] = _pan(W, cols).reshape(-1)
    return out


def pack_vec(inp):
    def col(v):
        return np.asarray(v, np.float32).reshape(-1, 128).T
    parts = [col(inp["ln1_g"][0]), col(inp["ln1_b"][0]), col(inp["ln2_g"][0]), col(inp["ln2_b"][0]),
             col(inp["ln3_g"][0]), col(inp["ln3_b"][0]), col(inp["b_gate"][0]),
             col(inp["conv_w"][0][0]), col(inp["conv_w"][0][1]), col(inp["conv_w"][0][2]),
             col(inp["conv_b"][0])]
    v = np.concatenate(parts, axis=1)
    assert v.shape == (128, NV)
    return np.ascontiguousarray(v)


V_LN = [(0, 8), (16, 24), (32, 40)]
V_BG = 48
V_CW = (64, 72, 80)
V_CB = 88


def pack_cst():
    c = np.zeros((128, NCST), np.float32)
    c[:, 0:128] = np.eye(128, dtype=np.float32)
    i = np.arange(128)[:, None]
    j = np.arange(128)[None, :]
    c[:, 128:256] = (i <= j).astype(np.float32)
    pb = np.zeros((2, 4, 8, 8), np.float32)
    for ti in range(2):
        for sub in range(4):
            blk = 2 * (ti + 2) + sub // 2
            pb[ti, sub, :, blk:] = NEG
    c[:, 256:768] = pb.reshape(1, 512)
    return c


def build_nc(ntiles=NT, stop=None):
    nc = bass.Bass("TRN2", target_bir_lowering=False)
    xT = nc.dram_tensor("xT", [NT, 128, 8 * T], F32, kind="ExternalInput").ap()
    wpk = nc.dram_tensor("wpk", [WTOT], F32, kind="ExternalInput").ap()
    vecd = nc.dram_tensor("vec", [128, NV], F32, kind="ExternalInput").ap()
    cstd = nc.dram_tensor("cst", [128, NCST], F32, kind="ExternalInput").ap()
    outT = nc.dram_tensor("outT", [D, S], F32, kind="ExternalOutput").ap()
    oTv = outT.rearrange("(c p) s -> p c s", p=128)

    with ExitStack() as es:
        def sb(name, shape, dt):
            return es.enter_context(nc.sbuf_tensor(name, shape, dt))

        P = Prog(nc)
        semnames = ["pe", "act", "dve", "pool", "xbl", "xfl", "vecl", "cstl"] + \
                   ["slot%d" % i for i in range(NSLOT)] + ["st%d" % c for c in range(8)]
        for k in semnames:
            P.add_sem(k, es.enter_context(nc.semaphore(k)))

        KT = sb("KT", [128, NH, S], BF16)
        VV = sb("VV", [128, S // 128, D], BF16)
        xf = sb("xf", [128, 8, T], F32)
        xb = sb("xb", [128, 8, T], BF16)
        hT = sb("hT", [128, 24, T], BF16)
        ring = [sb("ring%d" % i, [128, SLOTF], BF16) for i in range(NSLOT)]
        vec = sb("vecs", [128, NV], F32)
        cstf = sb("cstf", [128, NCST], F32)
        identb = sb("identb", [128, 128], BF16)
        trib = sb("trib", [128, 128], BF16)
        onesf = sb("onesf", [128, 128], F32)
        onesb = sb("onesb", [128, 128], BF16)
        epst = sb("epst", [128, 2], F32)
        dummy = sb("lndummy", [128, 1], F32)
        warm = sb("warm", [128, T], BF16)
        scr = [sb("scr%d" % i, [128, T], F32) for i in range(NSCR)]
        acc1 = sb("acc1", [128, T], F32)
        acc2 = sb("acc2", [128, T], F32)
        rstd = sb("rstd", [128, T], F32)
        ub = [sb("ub%d" % i, [128, T + 2], F32) for i in range(2)]
        cvs = [sb("cvs%d" % i, [128, T], F32) for i in range(4)]
        uhalo = sb("uhalo", [128, 8, 2], F32)
        PT = [sb("PT%d" % i, [128, T], BF16) for i in range(6)]
        planes = sb("planes", [128, 7, T], BF16)
        Gm = sb("Gm", [128, 256], F32)
        top8 = sb("top8", [128, 32, 8], F32)
        selb = sb("selb", [128, 32, 8], BF16)
        ksum = sb("ksum", [128, NH, 8], F32)
        kmT = sb("kmT", [128, NH, 8], BF16)
        pbk = [es.enter_context(nc.psum_tensor("pb%d" % i, [128, T], F32)) for i in range(8)]
        print("sbuf bytes remaining:", nc.sbuf_bytes_remaining)
        block = es.enter_context(nc.Block())

        ps_all = RR(range(8))
        ps_S = RR([0, 1])
        ps_O = RR([2, 3])
        ps_D = RR([4, 5])
        ps_P = RR([6, 7])
        scr_rr = RR(range(NSCR))
        pt_rr = RR(range(6))
        ub_rr = RR(range(2))
        cv_rr = RR(range(4))

        def PS(b):
            return ("ps", b)

        def SC(i):
            return ("scr", i)

        XB = [("xb", c) for c in range(8)]

        def HT(c):
            return ("hT", c)

        P.dma("sp", "vecl", lambda e: e.dma_start(out=vec[:], in_=vecd), writes=["vec"])
        P.dma("sp", "cstl", lambda e: e.dma_start(out=cstf[:], in_=cstd), writes=["cstf"])
        P.op("pool", lambda e: e.memset(onesf[:], 1.0), writes=["onesf"])
        P.op("pool", lambda e: e.memset(onesb[:], 1.0), writes=["onesb"])
        P.op("pool", lambda e: e.memset(warm[:], 1.0), writes=["warm"])
        P.op("pool", lambda e: e.memset(epst[:], EPS_S), writes=["epst"])
        P.op("pool", lambda e: e.memset(uhalo[:], 0.0), writes=[("uhalo", c) for c in range(8)])
        P.op("pool", lambda e: e.memset(kmT[:], 0.0), writes=[("kmT", h) for h in range(NH)])
        P.op("pool", lambda e: e.memset(ksum[:], 0.0), writes=[("ksum", h) for h in range(NH)])
        P.op("dve", lambda e: e.tensor_copy(out=identb[:], in_=cstf[:, 0:128]), reads=["cstf"], writes=["identb"])
        P.op("dve", lambda e: e.tensor_copy(out=trib[:], in_=cstf[:, 128:256]), reads=["cstf"], writes=["trib"])

        total_panels = ntiles * len(SCH)
        st = {"issued": 0, "used": 0}

        def issue_panel(gi):
            p = gi % len(SCH)
            slot = gi % NSLOT
            F = SCH_F[p]
            src = wpk[int(SCH_OFF[p]):int(SCH_OFF[p + 1])].rearrange("(p f) -> p f", p=128)

            def fn(e, slot=slot, F=F, src=src):
                return e.dma_start(out=ring[slot][:, 0:F], in_=src)
            gate = [("xb", 0)] if 1 <= gi < NSLOT else []
            P.dma("pool", "slot%d" % slot, fn, reads=gate, writes=[("w", slot)])

        def take_panel(kind, idx):
            gi = st["used"]
            p = gi % len(SCH)
            assert SCH[p][0] == kind and SCH[p][1] == idx, (SCH[p], kind, idx)
            base = st.get("hold")
            if base is None:
                base = gi
            while st["issued"] < min(base + NSLOT, total_panels):
                issue_panel(st["issued"])
                st["issued"] += 1
            st["used"] += 1
            slot = gi % NSLOT
            return ring[slot], ("w", slot)

        def mm_group(bank, W, col0, ncol_stride, rhs_chunks, nk, wtok, rtoks, m=128):
            def fn(e):
                ins = None
                for k in range(nk):
                    ins = e.matmul(pbk[bank][:, :], lhsT=W[:, k * ncol_stride + col0:k * ncol_stride + col0 + m],
                                   rhs=rhs_chunks(k), start=(k == 0), stop=(k == nk - 1))
                return ins
            P.op("pe", fn, reads=[wtok] + list(rtoks), writes=[PS(bank)])

        def wide_mm(specs, rhs_k, rtok_k, nk=8):
            for k in range(nk):
                def fn(e, k=k):
                    ins = None
                    for (bank, W, wt, col0, stride) in specs:
                        ins = e.matmul(pbk[bank][:, :], lhsT=W[:, k * stride + col0:k * stride + col0 + 128],
                                       rhs=rhs_k(k), start=(k == 0), stop=(k == nk - 1))
                    return ins
                P.op("pe", fn, reads=[sp_[2] for sp_ in specs] + [rtok_k(k)], writes=[PS(sp_[0]) for sp_ in specs])

        def ln_stats(n):
            xn = ("xf", n)
            if n == 0:
                P.op("act", lambda e: e.activation(out=acc2[:], in_=xf[:, 0, :], func=AF.Square), reads=[xn], writes=["acc2"])
                P.op("dve", lambda e: e.tensor_copy(out=acc1[:], in_=xf[:, 0, :]), reads=[xn], writes=["acc1"])
            else:
                si = scr_rr.next()
                P.op("act", lambda e: e.activation(out=scr[si][:], in_=xf[:, n, :], func=AF.Square), reads=[xn], writes=[SC(si)])
                P.op("dve", lambda e: e.tensor_tensor(out=acc1[:], in0=acc1[:], in1=xf[:, n, :], op=ALU.add), reads=["acc1", xn], writes=["acc1"])
                P.op("dve", lambda e: e.tensor_tensor(out=acc2[:], in0=acc2[:], in1=scr[si][:], op=ALU.add), reads=["acc2", SC(si)], writes=["acc2"])
                if n == 7:
                    P.op("act", lambda e: e.activation(out=dummy[:], in_=epst[:, 0:1], func=AF.Ln), reads=["epst"], writes=["dummy"])

        def layer_norm(li, want_bf16, after_chunk=None):
            g0, b0 = V_LN[li]
            b1 = ps_all.next()
            b2 = ps_all.next()
            P.op("pe", lambda e: e.matmul(pbk[b1][:, :], lhsT=onesf[:], rhs=acc1[:], start=True, stop=True),
                 reads=["onesf", "acc1"], writes=[PS(b1)])
            P.op("pe", lambda e: e.matmul(pbk[b2][:, :], lhsT=onesf[:], rhs=acc2[:], start=True, stop=True),
                 reads=["onesf", "acc2"], writes=[PS(b2)])
            NWARM = int(os.environ.get("K_NWARM", "24")) if li == 0 else int(os.environ.get("K_NWARM2", "17"))
            if li < 2 and NWARM:
                bw = ps_all.next()

                def wfn(e, bw=bw):
                    ins = None
                    for i in range(NWARM):
                        ins = e.matmul(pbk[bw][:, :], lhsT=identb[:], rhs=warm[:], start=True, stop=True)
                    return ins
                P.op("pe", wfn, reads=["identb", "warm"], writes=[PS(bw)])
            mean = acc1
            var = acc2
            si = scr_rr.next()
            P.op("dve", lambda e: e.tensor_scalar(out=mean[:], in0=pbk[b1][:, :], scalar1=1.0 / D, scalar2=None, op0=ALU.mult),
                 reads=[PS(b1)], writes=["acc1"])
            P.op("dve", lambda e: e.tensor_tensor(out=scr[si][:], in0=mean[:], in1=mean[:], op=ALU.mult), reads=["acc1"], writes=[SC(si)])
            P.op("dve", lambda e: e.scalar_tensor_tensor(out=var[:], in0=pbk[b2][:, :], scalar=1.0 / D, in1=scr[si][:],
                                                         op0=ALU.mult, op1=ALU.subtract),
                 reads=[PS(b2), SC(si)], writes=["acc2"])
            P.op("act", lambda e: e.activation(out=rstd[:], in_=var[:], func=AF.Ln, bias=epst[:, 0:1], scale=1.0),
                 reads=["acc2", "epst"], writes=["rstd"])
            P.op("act", lambda e: e.activation(out=rstd[:], in_=rstd[:], func=AF.Exp, scale=-0.5),
                 reads=["rstd"], writes=["rstd"])
            tsc = {}

            def t_op(c):
                s1 = scr_rr.next()
                tsc[c] = s1
                P.op("dve", lambda e, c=c, s1=s1: e.tensor_tensor(out=scr[s1][:], in0=xf[:, c, :], in1=mean[:], op=ALU.subtract),
                     reads=[("xf", c), "acc1"], writes=[SC(s1)])

            def affine(c, bf):
                dst = xb[:, c, :] if bf else xf[:, c, :]
                P.op("act", lambda e, c=c, dst=dst: e.activation(out=dst, in_=xf[:, c, :], func=AF.Identity,
                                                                 bias=vec[:, b0 + c:b0 + c + 1], scale=vec[:, g0 + c:g0 + c + 1]),
                     reads=[("xf", c), "vec"], writes=[XB[c] if bf else ("xf", c)])

            AHEAD = 3
            for c in range(AHEAD):
                t_op(c)
            for c in range(8):
                s1 = tsc[c]
                P.op("dve", lambda e, c=c, s1=s1: e.tensor_tensor(out=xf[:, c, :], in0=scr[s1][:], in1=rstd[:], op=ALU.mult),
                     reads=[SC(s1), "rstd"], writes=[("xf", c)])
                if c + AHEAD < 8:
                    t_op(c + AHEAD)
                if want_bf16:
                    affine(c, True)
                else:
                    affine(c, False)
                    if after_chunk is not None:
                        after_chunk(c)
            if want_bf16:
                for c in range(8):
                    affine(c, False)
                    if after_chunk is not None:
                        after_chunk(c)

        def ffn(which, after_up=None, wide=False):
            up, dn = ("UP1", "DN1") if which == 1 else ("UP2", "DN2")
            NW = 3 if wide else 0
            wbanks = {}
            if wide:
                specs = []
                st["hold"] = st["used"]
                for j in range(NW):
                    W, wt = take_panel(up, j)
                    bg = ps_all.next()
                    bu = ps_all.next()
                    wbanks[j] = (bg, bu)
                    specs.append((bg, W, wt, 0, 256))
                    specs.append((bu, W, wt, 128, 256))
                wide_mm(specs, lambda k: xb[:, k, :], lambda k: XB[k])
                st["hold"] = None
            for j in range(NJ):
                if j < NW:
                    bg, bu = wbanks[j]
                else:
                    W, wt = take_panel(up, j)
                    bg = ps_all.next()
                    bu = ps_all.next()
                    mm_group(bg, W, 0, 256, lambda k: xb[:, k, :], 8, wt, XB)
                    mm_group(bu, W, 128, 256, lambda k: xb[:, k, :], 8, wt, XB)
                si = scr_rr.next()
                P.op("act", lambda e, bg=bg, si=si: e.activation(out=scr[si][:], in_=pbk[bg][:, :], func=AF.Silu),
                     reads=[PS(bg)], writes=[SC(si)])
                P.op("dve", lambda e, bu=bu, si=si, j=j: e.tensor_tensor(out=hT[:, j, :], in0=scr[si][:], in1=pbk[bu][:, :], op=ALU.mult),
                     reads=[SC(si), PS(bu)], writes=[HT(j)])
                if j == 3 and not st.get("xf0_loaded"):
                    st["xf0_loaded"] = True
                    load_xf(0, gate=[HT(3)])
            if after_up is not None:
                after_up()
            for n in range(8):
                W, wt = take_panel(dn, n)
                by = ps_all.next()
                mm_group(by, W, 0, 128, lambda k: hT[:, k, :], NJ, wt, [HT(k) for k in range(NJ)])
                P.op("dve", lambda e, by=by, n=n: e.scalar_tensor_tensor(out=xf[:, n, :], in0=pbk[by][:, :], scalar=C_FFN, in1=xf[:, n, :],
                                                                        op0=ALU.mult, op1=ALU.add),
                     reads=[PS(by), ("xf", n)], writes=[("xf", n)])
                ln_stats(n)

        def mixer(t):
            tok0 = t * T
            kb = {}
            specs = []
            NWK = int(os.environ.get("K_NWK", "3"))
            if NWK:
                st["hold"] = st["used"]
            for i in range(NWK):
                W, wt = take_panel("KP", i)
                for hh in range(2):
                    b = ps_all.next()
                    kb[2 * i + hh] = b
                    specs.append((b, W, wt, hh * 128, 256))
            if NWK:
                wide_mm(specs, lambda k: xb[:, k, :], lambda k: XB[k])
                st["hold"] = None
            for i in range(4):
                if i >= NWK:
                    W, wt = take_panel("KP", i)
                for hh in range(2):
                    h = 2 * i + hh
                    if i < NWK:
                        b = kb[h]
                    else:
                        b = ps_all.next()
                        mm_group(b, W, hh * 128, 256, lambda k: xb[:, k, :], 8, wt, XB)
                    P.op("act", lambda e, b=b, h=h: e.copy(out=KT[:, h, tok0:tok0 + T], in_=pbk[b][:, :]),
                         reads=[PS(b)], writes=[("KT", h, t)])
                    P.op("dve", lambda e, b=b, h=h: e.tensor_reduce(out=ksum[:, h, 2 * t:2 * t + 2],
                                                                   in_=pbk[b][:, :].rearrange("p (a b) -> p a b", a=2),
                                                                   axis=AX.X, op=ALU.add),
                         reads=[PS(b)], writes=[("ksum", h)])
                    P.op("act", lambda e, h=h: e.mul(out=kmT[:, h, 2 * t:2 * t + 2], in_=ksum[:, h, 2 * t:2 * t + 2], mul=1.0 / 256.0),
                         reads=[("ksum", h)], writes=[("kmT", h)])
            for i in range(4):
                W, wt = take_panel("QP", i)
                for hh in range(2):
                    h = 2 * i + hh
                    b = ps_all.next()
                    mm_group(b, W, hh * 128, 256, lambda k: xb[:, k, :], 8, wt, XB)
                    P.op("dve", lambda e, b=b, h=h: e.tensor_copy(out=hT[:, h, :], in_=pbk[b][:, :]), reads=[PS(b)], writes=[HT(h)])
            need_sel = t >= 2
            if need_sel:
                def gfn(e):
                    ins = None
                    for sub in range(4):
                        for h in range(NH):
                            j = sub * 8 + h
                            ins = e.matmul(pbk[7][:, j * 8:(j + 1) * 8], lhsT=hT[:, h, sub * 128:(sub + 1) * 128],
                                           rhs=kmT[:, h, :], start=True, stop=True)
                    return ins
                P.op("pe", gfn, reads=[HT(h) for h in range(NH)] + [("kmT", h) for h in range(NH)], writes=[PS(7)])
                P.op("dve", lambda e: e.tensor_tensor(out=Gm[:], in0=pbk[7][:, 0:256], in1=cstf[:, 256 + (t - 2) * 256:256 + (t - 1) * 256], op=ALU.add),
                     reads=[PS(7), "cstf"], writes=["Gm"])
                for j in range(32):
                    P.op("dve", lambda e, j=j: e.max(out=top8[:, j, :], in_=Gm[:, j * 8:(j + 1) * 8]), reads=["Gm"], writes=["top8"])
                P.op("dve", lambda e: e.tensor_tensor(out=selb[:], in0=Gm[:].rearrange("p (a b) -> p a b", b=8),
                                                      in1=top8[:, :, 2:3].to_broadcast([128, 32, 8]), op=ALU.is_ge),
                     reads=["Gm", "top8"], writes=["selb"])
            init_planes = list(range(2 * t + 1)) if need_sel else []
            init_banks = RR([6, 7])

            def make_plane0(n):
                bp = init_banks.next()

                def pfn(e, n=n, bp=bp):
                    ins = None
                    for sub in range(4):
                        ins = e.matmul(pbk[bp][:, sub * 128:(sub + 1) * 128],
                                       lhsT=selb[:, sub * 8 + 0, n:n + 1].to_broadcast([128, 128]),
                                       rhs=identb[:], start=True, stop=True)
                    return ins
                P.op("pe", pfn, reads=["selb", "identb"], writes=[PS(bp)])
                P.op("dve", lambda e, n=n, bp=bp: e.tensor_copy(out=planes[:, n, :], in_=pbk[bp][:, :]), reads=[PS(bp)], writes=[("pl", n)])
            for i in range(4):
                W, wt = take_panel("VP", i)
                for sp in range(2):
                    b = ps_all.next()

                    def fn(e, b=b, sp=sp, W=W):
                        ins = None
                        for half in range(2):
                            sub = 2 * sp + half
                            for k in range(8):
                                ins = e.matmul(pbk[b][:, half * 256:(half + 1) * 256], lhsT=xb[:, k, sub * 128:(sub + 1) * 128],
                                               rhs=W[:, k * 256:(k + 1) * 256], start=(k == 0), stop=(k == 7))
                        return ins
                    P.op("pe", fn, reads=[wt] + XB, writes=[PS(b)])
                    eng = "act" if sp == 0 else "dve"

                    def ev(e, b=b, sp=sp, i=i, eng=eng):
                        o = VV[:, 4 * t + 2 * sp:4 * t + 2 * sp + 2, i * 256:(i + 1) * 256]
                        src = pbk[b][:, :].rearrange("p (a b) -> p a b", a=2)
                        if eng == "act":
                            return e.copy(out=o, in_=src)
                        return e.tensor_copy(out=o, in_=src)
                    P.op(eng, ev, reads=[PS(b)], writes=[("VV", t, i, sp)])
                    vg = 2 * i + sp
                    if vg >= 4:
                        for _ in range(2):
                            if init_planes:
                                make_plane0(init_planes.pop(0))
            if stop == "kqv":
                return 0
            while init_planes:
                make_plane0(init_planes.pop(0))
            if stop == "kqv":
                return 0
            nkc = 4 * (t + 1)
            if need_sel:
                conv_banks = RR([7])
                plane_banks = RR([6])
            else:
                conv_banks = RR([6, 7])
                plane_banks = None

            def conv_gen():
                for c in range(8):
                    W, wt = take_panel("CP", c)
                    bh = conv_banks.next()
                    mm_group(bh, W, 0, 384, lambda k: xb[:, k, :], 8, wt, XB)
                    sh = cv_rr.next()
                    P.op("dve", lambda e, bh=bh, sh=sh: e.tensor_copy(out=cvs[sh][:], in_=pbk[bh][:, :]), reads=[PS(bh)], writes=[("cvs", sh)])
                    yield
                    bc = conv_banks.next()
                    mm_group(bc, W, 256, 384, lambda k: xb[:, k, :], 8, wt, XB)
                    ui = ub_rr.next()
                    U = ("ub", ui)
                    P.op("pool", lambda e, ui=ui, c=c: e.tensor_copy(out=ub[ui][:, 0:2], in_=uhalo[:, c, :]), reads=[("uhalo", c)], writes=[U])
                    P.op("dve", lambda e, ui=ui, sh=sh, bc=bc: e.tensor_tensor(out=ub[ui][:, 2:T + 2], in0=cvs[sh][:], in1=pbk[bc][:, :], op=ALU.mult),
                         reads=[("cvs", sh), PS(bc)], writes=[U])
                    P.op("pool", lambda e, ui=ui, c=c: e.tensor_copy(out=uhalo[:, c, :], in_=ub[ui][:, T:T + 2]), reads=[U], writes=[("uhalo", c)])
                    yield
                    bb = conv_banks.next()
                    mm_group(bb, W, 128, 384, lambda k: xb[:, k, :], 8, wt, XB)
                    sa = cv_rr.next()
                    A = ("cvs", sa)
                    P.op("dve", lambda e, ui=ui, sa=sa, c=c: e.tensor_scalar(out=cvs[sa][:], in0=ub[ui][:, 2:T + 2],
                                                                             scalar1=vec[:, V_CW[2] + c:V_CW[2] + c + 1], scalar2=vec[:, V_CB + c:V_CB + c + 1],
                                                                             op0=ALU.mult, op1=ALU.add),
                         reads=[U, "vec"], writes=[A])
                    yield
                    P.op("dve", lambda e, ui=ui, sa=sa, c=c: e.scalar_tensor_tensor(out=cvs[sa][:], in0=ub[ui][:, 1:T + 1], scalar=vec[:, V_CW[1] + c:V_CW[1] + c + 1],
                                                                                    in1=cvs[sa][:], op0=ALU.mult, op1=ALU.add),
                         reads=[U, A, "vec"], writes=[A])
                    P.op("dve", lambda e, ui=ui, sa=sa, c=c: e.scalar_tensor_tensor(out=cvs[sa][:], in0=ub[ui][:, 0:T], scalar=vec[:, V_CW[0] + c:V_CW[0] + c + 1],
                                                                                    in1=cvs[sa][:], op0=ALU.mult, op1=ALU.add),
                         reads=[U, A, "vec"], writes=[A])
                    yield
                    P.op("dve", lambda e, sa=sa, bb=bb, c=c: e.tensor_tensor(out=hT[:, 16 + c, :], in0=cvs[sa][:], in1=pbk[bb][:, :], op=ALU.mult),
                         reads=[A, PS(bb)], writes=[HT(16 + c)])
                    yield

            def make_plane(h, n):
                bp = plane_banks.next()

                def pfn(e, h=h, n=n, bp=bp):
                    ins = None
                    for sub in range(4):
                        ins = e.matmul(pbk[bp][:, sub * 128:(sub + 1) * 128],
                                       lhsT=selb[:, sub * 8 + h, n:n + 1].to_broadcast([128, 128]),
                                       rhs=identb[:], start=True, stop=True)
                    return ins
                P.op("pe", pfn, reads=["selb", "identb"], writes=[PS(bp)])
                P.op("dve", lambda e, n=n, bp=bp: e.tensor_copy(out=planes[:, n, :], in_=pbk[bp][:, :]), reads=[PS(bp)], writes=[("pl", n)])

            hb = {}

            def banks_of(h):
                if h not in hb:
                    hb[h] = (ps_O.next(), ps_D.next())
                return hb[h]

            def emit_S(h, kc):
                tt = kc // 4
                q0 = (kc - 4 * t) * 128 if tt == t else 0
                bs = ps_S.next()
                pi = pt_rr.next()
                P.op("pe", lambda e: e.matmul(pbk[bs][:, q0:T], lhsT=KT[:, h, kc * 128:(kc + 1) * 128],
                                              rhs=hT[:, h, q0:T], start=True, stop=True),
                     reads=[("KT", h, tt), HT(h)], writes=[PS(bs)])
                P.op("act", lambda e: e.activation(out=PT[pi][:, q0:T], in_=pbk[bs][:, q0:T], func=AF.Exp, scale=SCALE),
                     reads=[PS(bs)], writes=[("PT", pi)])
                n = kc // 2
                if tt == t:
                    P.op("dve", lambda e: e.tensor_tensor(out=PT[pi][:, q0:q0 + 128], in0=PT[pi][:, q0:q0 + 128], in1=trib[:], op=ALU.mult),
                         reads=[("PT", pi), "trib"], writes=[("PT", pi)])
                    if need_sel and n == 2 * t:
                        P.op("dve", lambda e: e.tensor_tensor(out=PT[pi][:, 256:T], in0=PT[pi][:, 256:T], in1=planes[:, n, 256:T], op=ALU.mult),
                             reads=[("PT", pi), ("pl", n)], writes=[("PT", pi)])
                elif need_sel:
                    P.op("dve", lambda e: e.tensor_tensor(out=PT[pi][:, :], in0=PT[pi][:, :], in1=planes[:, n, :], op=ALU.mult),
                         reads=[("PT", pi), ("pl", n)], writes=[("PT", pi)])
                if need_sel and h + 1 < NH and n <= 2 * t and kc == 2 * n + 1:
                    make_plane(h + 1, n)
                return pi, q0, tt

            def emit_PV(h, kc, pi, q0, tt):
                bo, bd = banks_of(h)
                first = kc == 0
                last = kc == nkc - 1
                sub = kc % 4

                def fn(e):
                    e.matmul(pbk[bo][:, q0:T], lhsT=VV[:, kc, h * 128:(h + 1) * 128], rhs=PT[pi][:, q0:T],
                             start=first, stop=last, skip_group_check=True)
                    return e.matmul(pbk[bd][:, q0:T], lhsT=onesb[:], rhs=PT[pi][:, q0:T],
                                    start=first, stop=last, skip_group_check=True)
                P.op("pe", fn, reads=[("VV", tt, h // 2, sub // 2), ("PT", pi), "onesb"], writes=[PS(bo), PS(bd)])
                if last:
                    si = scr_rr.next()
                    P.op("act", lambda e: e.activation(out=scr[si][:], in_=pbk[bd][:, :], func=AF.Ln), reads=[PS(bd)], writes=[SC(si)])
                    P.op("act", lambda e: e.activation(out=scr[si][:], in_=scr[si][:], func=AF.Exp, scale=-1.0), reads=[SC(si)], writes=[SC(si)])
                    P.op("dve", lambda e: e.tensor_tensor(out=hT[:, 8 + h, :], in0=pbk[bo][:, :], in1=scr[si][:], op=ALU.mult),
                         reads=[PS(bo), SC(si)], writes=[HT(8 + h)])

            jobs = [(h, kc) for h in range(NH) for kc in range(nkc)]
            conv_it = conv_gen()
            NPIECE = 8 * 5
            every = max(1, len(jobs) // NPIECE)
            LOOK = 5
            sres = {}
            for step in range(len(jobs) + LOOK):
                if step < len(jobs):
                    sres[step] = emit_S(*jobs[step])
                if step >= LOOK:
                    j = step - LOOK
                    emit_PV(*jobs[j], *sres[j])
                if step % every == 0:
                    next(conv_it, None)
            for _ in conv_it:
                pass
            if stop == "attn":
                return 8
            if stop == "conv":
                return 16
            for c in range(8):
                Wg, wtg = take_panel("GA", c)
                bga = ps_all.next()
                bgc = ps_all.next()
                mm_group(bga, Wg, 0, 256, lambda k: xb[:, k, :], 8, wtg, XB)
                mm_group(bgc, Wg, 128, 256, lambda k: xb[:, k, :], 8, wtg, XB)
                Wp, wtp = take_panel("PP", c)
                bya = ps_all.next()
                byc = ps_all.next()
                mm_group(bya, Wp, 0, 256, lambda k: hT[:, 8 + k, :], 8, wtp, [HT(8 + k) for k in range(8)])
                mm_group(byc, Wp, 128, 256, lambda k: hT[:, 16 + k, :], 8, wtp, [HT(16 + k) for k in range(8)])
                s1 = scr_rr.next()
                s2 = scr_rr.next()
                P.op("act", lambda e, bga=bga, s1=s1, c=c: e.activation(out=scr[s1][:], in_=pbk[bga][:, :], func=AF.Sigmoid,
                                                                        bias=vec[:, V_BG + c:V_BG + c + 1], scale=1.0),
                     reads=[PS(bga), "vec"], writes=[SC(s1)])
                P.op("act", lambda e, bgc=bgc, s2=s2, c=c: e.activation(out=scr[s2][:], in_=pbk[bgc][:, :], func=AF.Sigmoid,
                                                                        bias=vec[:, V_BG + 8 + c:V_BG + 8 + c + 1], scale=1.0),
                     reads=[PS(bgc), "vec"], writes=[SC(s2)])
                P.op("dve", lambda e, bya=bya, s1=s1: e.tensor_tensor(out=scr[s1][:], in0=scr[s1][:], in1=pbk[bya][:, :], op=ALU.mult),
                     reads=[SC(s1), PS(bya)], writes=[SC(s1)])
                P.op("dve", lambda e, byc=byc, s2=s2: e.tensor_tensor(out=scr[s2][:], in0=scr[s2][:], in1=pbk[byc][:, :], op=ALU.mult),
                     reads=[SC(s2), PS(byc)], writes=[SC(s2)])
                P.op("dve", lambda e, s1=s1, s2=s2, c=c: e.tensor_tensor(out=hT[:, c, :], in0=scr[s1][:], in1=scr[s2][:], op=ALU.add),
                     reads=[SC(s1), SC(s2)], writes=[HT(c)])
            if stop == "merge":
                return 0
            for i in range(4):
                W, wt = take_panel("WO", i)
                for hh in range(2):
                    n = 2 * i + hh
                    b = ps_all.next()
                    mm_group(b, W, hh * 128, 256, lambda k: hT[:, k, :], 8, wt, [HT(k) for k in range(8)])
                    P.op("dve", lambda e, b=b, n=n: e.scalar_tensor_tensor(out=xf[:, n, :], in0=pbk[b][:, :], scalar=C_MIX, in1=xf[:, n, :],
                                                                          op0=ALU.mult, op1=ALU.add),
                         reads=[PS(b), ("xf", n)], writes=[("xf", n)])
                    ln_stats(n)

        def load_xb(t):
            P.dma("pool", "xbl", lambda e: e.dma_start(out=xb[:].rearrange("p a b -> p (a b)"), in_=xT[t]), writes=XB)

        def load_xf(t, gate=()):
            P.dma("sp", "xfl", lambda e: e.dma_start(out=xf[:].rearrange("p a b -> p (a b)"), in_=xT[t]), reads=list(gate),
                  writes=[("xf", c) for c in range(8)])

        load_xb(0)

        def dump(t, from_hT=None):
            for c in range(8):
                if from_hT is not None:
                    P.op("dve", lambda e, c=c: e.tensor_copy(out=xf[:, c, :], in_=hT[:, from_hT + c, :]),
                         reads=[HT(from_hT + c)], writes=[("xf", c)])
                P.dma("sp", "st%d" % c, lambda e, c=c: e.dma_start(out=oTv[:, c, t * T:(t + 1) * T], in_=xf[:, c, :]), reads=[("xf", c)])

        for t in range(ntiles):
            last_t = t + 1 == ntiles
            ffn(1)
            if stop == "ffn1" and last_t:
                dump(t)
                break
            layer_norm(0, True)
            if stop == "ln1" and last_t:
                dump(t)
                break
            r = mixer(t)
            if r is not None:
                if last_t:
                    dump(t, from_hT=r)
                    break
                else:
                    raise ValueError("mixer stop only with single tile")
            if stop == "wout" and last_t:
                dump(t)
                break
            layer_norm(1, True)
            if stop == "ln2" and last_t:
                dump(t)
                break
            ffn(2, after_up=(partial(load_xb, t + 1) if t + 1 < ntiles else None), wide=True)

            def store(c, t=t):
                P.dma("sp", "st%d" % c, lambda e: e.dma_start(out=oTv[:, c, t * T:(t + 1) * T], in_=xf[:, c, :]), reads=[("xf", c)])
            layer_norm(2, False, after_chunk=store)
            if t + 1 < ntiles:
                load_xf(t + 1)
        P.final_wait("sp", ["st%d" % c for c in range(8)])
        P.emit(block)
        print("ops:", {k: len(v) for k, v in P.streams.items()})
    return nc


_CACHE = {}


def kernel(**inputs):
    x = np.asarray(inputs["x"], np.float32)
    B = x.shape[0]
    wp = pack_weights(inputs)
    vec = pack_vec(inputs)
    cst = pack_cst()
    if "nc" not in _CACHE:
        _CACHE["nc"] = build_nc()
    nc = _CACHE["nc"]
    in_maps = []
    for b in range(B):
        xt = np.ascontiguousarray(x[b].reshape(NT, T, 8, 128).transpose(0, 3, 2, 1)).reshape(NT, 128, 8 * T)
        in_maps.append({"xT": xt, "wpk": wp, "vec": vec, "cst": cst})
    res = run_bass_kernel_spmd(nc, in_maps, core_ids=list(range(B)))
    out = np.stack([np.ascontiguousarray(r["outT"].T) for r in res.results], axis=0)
    return out.astype(np.float32)
```

```python
import os
from contextlib import ExitStack
from functools import partial

import numpy as np
import concourse.bass as bass
import concourse.mybir as mybir
from concourse.bass_utils import run_bass_kernel_spmd

F32 = mybir.dt.float32
BF16 = mybir.dt.bfloat16
ALU = mybir.AluOpType
AF = mybir.ActivationFunctionType
AX = mybir.AxisListType

D = 1024
S = 2048
T = 512
NT = S // T
DFF = 2816
NJ = DFF // 128
HD = 128
NH = 8
ALPHA = 2.0 ** 0.25
LN_EPS = 1e-5
EPS_S = LN_EPS / (ALPHA * ALPHA)
C_FFN = 0.5 / ALPHA
C_MIX = 1.0 / ALPHA
SCALE = HD ** -0.5
NSLOT = int(os.environ.get("K_NSLOT", "6"))
SLOTF = 3072
NSCR = 8
NEG = -1.0e30

COMPUTE = ("pe", "act", "dve", "pool")


class Prog:
    def __init__(self, nc):
        self.nc = nc
        self.streams = {e: [] for e in ("pe", "act", "dve", "pool", "sp")}
        self.semh = {}
        self.cnt = {}
        self.seen = {e: {} for e in self.streams}
        self.lastw = {}
        self.readers = {}
        self.pos = {e: 0 for e in self.streams}
        self.pos_of = {}

    def add_sem(self, key, handle):
        self.semh[key] = handle
        self.cnt[key] = 0

    def _collect(self, eng, reads, writes):
        deps = {}

        def add(d, raw):
            if d is None:
                return
            k, v = d
            if k == eng and eng == "pe":
                return
            if v > deps.get(k, 0):
                deps[k] = v

        for t in reads:
            add(self.lastw.get(t), True)
        for t in writes:
            add(self.lastw.get(t), False)
            for k, v in self.readers.get(t, {}).items():
                add((k, v), False)
        need = []
        for k, v in deps.items():
            if v > self.seen[eng].get(k, 0):
                self.seen[eng][k] = v
                need.append((k, v))
        return need

    def op(self, eng, fn, reads=(), writes=()):
        psr = [t for t in reads if isinstance(t, tuple) and t[0] == "ps"]
        if psr:
            reads = [t for t in reads if t not in psr]
            writes = list(writes) + psr
            raw_ps = psr
        else:
            raw_ps = ()
        need = self._collect(eng, list(reads) + list(raw_ps), writes)
        self.cnt[eng] += 1
        v = self.cnt[eng]
        self.pos[eng] += 1
        self.pos_of[(eng, v)] = self.pos[eng]
        self.streams[eng].append((need, fn, (eng, 1)))
        for t in reads:
            self.readers.setdefault(t, {})[eng] = v
        for t in writes:
            self.lastw[t] = (eng, v)
            self.readers[t] = {}

    def dma(self, queue, semkey, fn, reads=(), writes=()):
        need = self._collect(queue, reads, writes)
        self.cnt[semkey] += 16
        v = self.cnt[semkey]
        self.pos[queue] += 1
        self.streams[queue].append((need, fn, (semkey, 16)))
        for t in reads:
            self.readers.setdefault(t, {})[semkey] = v
        for t in writes:
            self.lastw[t] = (semkey, v)
            self.readers[t] = {}

    def final_wait(self, eng, semkeys):
        need = [(k, self.cnt[k]) for k in semkeys if self.cnt[k] > 0]
        self.streams[eng].append((need, None, None))

    def emit(self, block):
        P = self

        def run(engobj, name):
            for need, fn, inc in P.streams[name]:
                for k, v in need:
                    engobj.wait_ge(P.semh[k], v)
                if fn is None:
                    continue
                ins = fn(engobj)
                ins.then_inc(P.semh[inc[0]], inc[1])

        @block.tensor
        def _(e):
            run(e, "pe")

        @block.scalar
        def _(e):
            run(e, "act")

        @block.vector
        def _(e):
            run(e, "dve")

        @block.gpsimd
        def _(e):
            run(e, "pool")

        @block.sync
        def _(e):
            run(e, "sp")


class RR:
    def __init__(self, items):
        self.items = list(items)
        self.i = 0

    def next(self):
        x = self.items[self.i % len(self.items)]
        self.i += 1
        return x


def panel_schedule():
    sch = []
    for j in range(NJ):
        sch.append(("UP1", j, 8, 256))
    for n in range(8):
        sch.append(("DN1", n, NJ, 128))
    for i in range(4):
        sch.append(("KP", i, 8, 256))
    for i in range(4):
        sch.append(("QP", i, 8, 256))
    for i in range(4):
        sch.append(("VP", i, 8, 256))
    for c in range(8):
        sch.append(("CP", c, 8, 384))
    for c in range(8):
        sch.append(("GA", c, 8, 256))
        sch.append(("PP", c, 8, 256))
    for i in range(4):
        sch.append(("WO", i, 8, 256))
    for j in range(NJ):
        sch.append(("UP2", j, 8, 256))
    for n in range(8):
        sch.append(("DN2", n, NJ, 128))
    return sch


SCH = panel_schedule()
SCH_F = [kc * nc_ for (_, _, kc, nc_) in SCH]
SCH_OFF = np.concatenate([[0], np.cumsum([128 * f for f in SCH_F])]).astype(np.int64)
WTOT = int(SCH_OFF[-1])
NV = 96
NCST = 128 + 128 + 512


def _pan(W, cols):
    sub = W[:, cols]
    K, n = sub.shape
    kc = K // 128
    return np.ascontiguousarray(sub.reshape(kc, 128, n).transpose(1, 0, 2)).reshape(128, kc * n)


def pack_weights(inp):
    out = np.empty((WTOT,), np.float32)
    w_in = inp["w_in"][0]
    ar = np.arange
    for p, (kind, idx, kc, ncol) in enumerate(SCH):
        if kind in ("UP1", "UP2"):
            W = inp["ffn1_w_up" if kind == "UP1" else "ffn2_w_up"][0]
            cols = np.concatenate([ar(idx * 128, idx * 128 + 128), ar(DFF + idx * 128, DFF + idx * 128 + 128)])
        elif kind in ("DN1", "DN2"):
            W = inp["ffn1_w_down" if kind == "DN1" else "ffn2_w_down"][0]
            cols = ar(idx * 128, idx * 128 + 128)
        elif kind == "QP":
            W, cols = w_in, ar(idx * 256, idx * 256 + 256)
        elif kind == "KP":
            W, cols = w_in, ar(1024 + idx * 256, 1024 + idx * 256 + 256)
        elif kind == "VP":
            W, cols = w_in, ar(2048 + idx * 256, 2048 + idx * 256 + 256)
        elif kind == "CP":
            W = w_in
            cols = np.concatenate([ar(3072 + idx * 128, 3072 + idx * 128 + 128),
                                   ar(4096 + idx * 128, 4096 + idx * 128 + 128),
                                   ar(5120 + idx * 128, 5120 + idx * 128 + 128)])
        elif kind == "GA":
            W = w_in
            cols = np.concatenate([ar(6144 + idx * 128, 6144 + idx * 128 + 128),
                                   ar(7168 + idx * 128, 7168 + idx * 128 + 128)])
        elif kind == "PP":
            W = np.concatenate([inp["w_proj_attn"][0][:, idx * 128:(idx + 1) * 128],
                                inp["w_proj_conv"][0][:, idx * 128:(idx + 1) * 128]], axis=1)
            cols = ar(0, 256)
        elif kind == "WO":
            W, cols = inp["w_out"][0], ar(idx * 256, idx * 256 + 256)
        else:
            raise ValueError(kind)
        out[SCH_OFF[p]:SCH_OFF[p + 1]] = _pan(W, cols).reshape(-1)
    return out


def pack_vec(inp):
    def col(v):
        return np.asarray(v, np.float32).reshape(-1, 128).T
    parts = [col(inp["ln1_g"][0]), col(inp["ln1_b"][0]), col(inp["ln2_g"][0]), col(inp["ln2_b"][0]),
             col(inp["ln3_g"][0]), col(inp["ln3_b"][0]), col(inp["b_gate"][0]),
             col(inp["conv_w"][0][0]), col(inp["conv_w"][0][1]), col(inp["conv_w"][0][2]),
             col(inp["conv_b"][0])]
    v = np.concatenate(parts, axis=1)
    assert v.shape == (128, NV)
    return np.ascontiguousarray(v)


V_LN = [(0, 8), (16, 24), (32, 40)]
V_BG = 48
V_CW = (64, 72, 80)
V_CB = 88


def pack_cst():
    c = np.zeros((128, NCST), np.float32)
    c[:, 0:128] = np.eye(128, dtype=np.float32)
    i = np.arange(128)[:, None]
    j = np.arange(128)[None, :]
    c[:, 128:256] = (i <= j).astype(np.float32)
    pb = np.zeros((2, 4, 8, 8), np.float32)
    for ti in range(2):
        for sub in range(4):
            blk = 2 * (ti + 2) + sub // 2
            pb[ti, sub, :, blk:] = NEG
    c[:, 256:768] = pb.reshape(1, 512)
    return c


def build_nc(ntiles=NT, stop=None):
    nc = bass.Bass("TRN2", target_bir_lowering=False)
    xT = nc.dram_tensor("xT", [NT, 128, 8 * T], F32, kind="ExternalInput").ap()
    wpk = nc.dram_tensor("wpk", [WTOT], F32, kind="ExternalInput").ap()
    vecd = nc.dram_tensor("vec", [128, NV], F32, kind="ExternalInput").ap()
    cstd = nc.dram_tensor("cst", [128, NCST], F32, kind="ExternalInput").ap()
    outT = nc.dram_tensor("outT", [D, S], F32, kind="ExternalOutput").ap()
    oTv = outT.rearrange("(c p) s -> p c s", p=128)

    with ExitStack() as es:
        def sb(name, shape, dt):
            return es.enter_context(nc.sbuf_tensor(name, shape, dt))

        P = Prog(nc)
        semnames = ["pe", "act", "dve", "pool", "xbl", "xfl", "vecl", "cstl"] + \
                   ["slot%d" % i for i in range(NSLOT)] + ["st%d" % c for c in range(8)]
        for k in semnames:
            P.add_sem(k, es.enter_context(nc.semaphore(k)))

        KT = sb("KT", [128, NH, S], BF16)
        VV = sb("VV", [128, S // 128, D], BF16)
        xf = sb("xf", [128, 8, T], F32)
        xb = sb("xb", [128, 8, T], BF16)
        hT = sb("hT", [128, 24, T], BF16)
        ring = [sb("ring%d" % i, [128, SLOTF], BF16) for i in range(NSLOT)]
        vec = sb("vecs", [128, NV], F32)
        cstf = sb("cstf", [128, NCST], F32)
        identb = sb("identb", [128, 128], BF16)
        trib = sb("trib", [128, 128], BF16)
        onesf = sb("onesf", [128, 128], F32)
        onesb = sb("onesb", [128, 128], BF16)
        epst = sb("epst", [128, 2], F32)
        dummy = sb("lndummy", [128, 1], F32)
        warm = sb("warm", [128, T], BF16)
        scr = [sb("scr%d" % i, [128, T], F32) for i in range(NSCR)]
        acc1 = sb("acc1", [128, T], F32)
        acc2 = sb("acc2", [128, T], F32)
        rstd = sb("rstd", [128, T], F32)
        ub = [sb("ub%d" % i, [128, T + 2], F32) for i in range(2)]
        cvs = [sb("cvs%d" % i, [128, T], F32) for i in range(4)]
        uhalo = sb("uhalo", [128, 8, 2], F32)
        PT = [sb("PT%d" % i, [128, T], BF16) for i in range(5)]
        planes = sb("planes", [128, 7, T], BF16)
        Gm = sb("Gm", [128, 256], F32)
        top8 = sb("top8", [128, 32, 8], F32)
        selb = sb("selb", [128, 32, 8], BF16)
        ksum = sb("ksum", [128, NH, 8], F32)
        kmT = sb("kmT", [128, NH, 8], BF16)
        pbk = [es.enter_context(nc.psum_tensor("pb%d" % i, [128, T], F32)) for i in range(8)]
        print("sbuf bytes remaining:", nc.sbuf_bytes_remaining)
        block = es.enter_context(nc.Block())

        ps_all = RR(range(8))
        ps_S = RR([0, 1])
        ps_O = RR([2, 3])
        ps_D = RR([4, 5])
        ps_P = RR([6, 7])
        scr_rr = RR(range(NSCR))
        pt_rr = RR(range(5))
        ub_rr = RR(range(2))
        cv_rr = RR(range(4))

        def PS(b):
            return ("ps", b)

        def SC(i):
            return ("scr", i)

        XB = [("xb", c) for c in range(8)]

        def HT(c):
            return ("hT", c)

        P.dma("sp", "vecl", lambda e: e.dma_start(out=vec[:], in_=vecd), writes=["vec"])
        P.dma("sp", "cstl", lambda e: e.dma_start(out=cstf[:], in_=cstd), writes=["cstf"])
        P.op("pool", lambda e: e.memset(onesf[:], 1.0), writes=["onesf"])
        P.op("pool", lambda e: e.memset(onesb[:], 1.0), writes=["onesb"])
        P.op("pool", lambda e: e.memset(warm[:], 1.0), writes=["warm"])
        P.op("pool", lambda e: e.memset(epst[:], EPS_S), writes=["epst"])
        P.op("pool", lambda e: e.memset(uhalo[:], 0.0), writes=[("uhalo", c) for c in range(8)])
        P.op("pool", lambda e: e.memset(kmT[:], 0.0), writes=[("kmT", h) for h in range(NH)])
        P.op("pool", lambda e: e.memset(ksum[:], 0.0), writes=[("ksum", h) for h in range(NH)])
        P.op("dve", lambda e: e.tensor_copy(out=identb[:], in_=cstf[:, 0:128]), reads=["cstf"], writes=["identb"])
        P.op("dve", lambda e: e.tensor_copy(out=trib[:], in_=cstf[:, 128:256]), reads=["cstf"], writes=["trib"])

        total_panels = ntiles * len(SCH)
        st = {"issued": 0, "used": 0}

        def issue_panel(gi):
            p = gi % len(SCH)
            slot = gi % NSLOT
            F = SCH_F[p]
            src = wpk[int(SCH_OFF[p]):int(SCH_OFF[p + 1])].rearrange("(p f) -> p f", p=128)

            def fn(e, slot=slot, F=F, src=src):
                return e.dma_start(out=ring[slot][:, 0:F], in_=src)
            gate = [("xb", 0)] if 1 <= gi < NSLOT else []
            P.dma("pool", "slot%d" % slot, fn, reads=gate, writes=[("w", slot)])

        def take_panel(kind, idx):
            gi = st["used"]
            p = gi % len(SCH)
            assert SCH[p][0] == kind and SCH[p][1] == idx, (SCH[p], kind, idx)
            base = st.get("hold")
            if base is None:
                base = gi
            while st["issued"] < min(base + NSLOT, total_panels):
                issue_panel(st["issued"])
                st["issued"] += 1
            st["used"] += 1
            slot = gi % NSLOT
            return ring[slot], ("w", slot)

        def mm_group(bank, W, col0, ncol_stride, rhs_chunks, nk, wtok, rtoks, m=128):
            def fn(e):
                ins = None
                for k in range(nk):
                    ins = e.matmul(pbk[bank][:, :], lhsT=W[:, k * ncol_stride + col0:k * ncol_stride + col0 + m],
                                   rhs=rhs_chunks(k), start=(k == 0), stop=(k == nk - 1))
                return ins
            P.op("pe", fn, reads=[wtok] + list(rtoks), writes=[PS(bank)])

        def wide_mm(specs, rhs_k, rtok_k, nk=8):
            for k in range(nk):
                def fn(e, k=k):
                    ins = None
                    for (bank, W, wt, col0, stride) in specs:
                        ins = e.matmul(pbk[bank][:, :], lhsT=W[:, k * stride + col0:k * stride + col0 + 128],
                                       rhs=rhs_k(k), start=(k == 0), stop=(k == nk - 1))
                    return ins
                P.op("pe", fn, reads=[sp_[2] for sp_ in specs] + [rtok_k(k)], writes=[PS(sp_[0]) for sp_ in specs])

        def ln_stats(n):
            xn = ("xf", n)
            if n == 0:
                P.op("act", lambda e: e.activation(out=acc2[:], in_=xf[:, 0, :], func=AF.Square), reads=[xn], writes=["acc2"])
                P.op("dve", lambda e: e.tensor_copy(out=acc1[:], in_=xf[:, 0, :]), reads=[xn], writes=["acc1"])
            else:
                si = scr_rr.next()
                P.op("act", lambda e: e.activation(out=scr[si][:], in_=xf[:, n, :], func=AF.Square), reads=[xn], writes=[SC(si)])
                P.op("dve", lambda e: e.tensor_tensor(out=acc1[:], in0=acc1[:], in1=xf[:, n, :], op=ALU.add), reads=["acc1", xn], writes=["acc1"])
                if n < 7:
                    P.op("dve", lambda e: e.tensor_tensor(out=acc2[:], in0=acc2[:], in1=scr[si][:], op=ALU.add), reads=["acc2", SC(si)], writes=["acc2"])
                else:
                    st["sq7"] = si
                if n == 7:
                    P.op("act", lambda e: e.activation(out=dummy[:], in_=epst[:, 0:1], func=AF.Ln), reads=["epst"], writes=["dummy"])

        def layer_norm(li, want_bf16, after_chunk=None):
            g0, b0 = V_LN[li]
            b1 = ps_all.next()
            b2 = ps_all.next()
            sq7 = st["sq7"]
            P.op("pe", lambda e: e.matmul(pbk[b2][:, :], lhsT=onesf[:], rhs=acc2[:], start=True, stop=False),
                 reads=["onesf", "acc2"], writes=[PS(b2)])
            P.op("pe", lambda e: e.matmul(pbk[b1][:, :], lhsT=onesf[:], rhs=acc1[:], start=True, stop=True),
                 reads=["onesf", "acc1"], writes=[PS(b1)])
            P.op("pe", lambda e: e.matmul(pbk[b2][:, :], lhsT=onesf[:], rhs=scr[sq7][:], start=False, stop=True),
                 reads=["onesf", SC(sq7)], writes=[PS(b2)])
            NWARM = int(os.environ.get("K_NWARM", "24")) if li == 0 else int(os.environ.get("K_NWARM2", "17"))
            if li < 2 and NWARM:
                bw = ps_all.next()

                def wfn(e, bw=bw):
                    ins = None
                    for i in range(NWARM):
                        ins = e.matmul(pbk[bw][:, :], lhsT=identb[:], rhs=warm[:], start=True, stop=True)
                    return ins
                P.op("pe", wfn, reads=["identb", "warm"], writes=[PS(bw)])
            mean = acc1
            var = acc2
            si = scr_rr.next()
            P.op("dve", lambda e: e.tensor_scalar(out=mean[:], in0=pbk[b1][:, :], scalar1=1.0 / D, scalar2=None, op0=ALU.mult),
                 reads=[PS(b1)], writes=["acc1"])
            P.op("dve", lambda e: e.tensor_tensor(out=scr[si][:], in0=mean[:], in1=mean[:], op=ALU.mult), reads=["acc1"], writes=[SC(si)])
            P.op("dve", lambda e: e.scalar_tensor_tensor(out=var[:], in0=pbk[b2][:, :], scalar=1.0 / D, in1=scr[si][:],
                                                         op0=ALU.mult, op1=ALU.subtract),
                 reads=[PS(b2), SC(si)], writes=["acc2"])
            P.op("act", lambda e: e.activation(out=rstd[:], in_=var[:], func=AF.Ln, bias=epst[:, 0:1], scale=1.0),
                 reads=["acc2", "epst"], writes=["rstd"])
            P.op("act", lambda e: e.activation(out=rstd[:], in_=rstd[:], func=AF.Exp, scale=-0.5),
                 reads=["rstd"], writes=["rstd"])
            tsc = {}

            def t_op(c):
                s1 = scr_rr.next()
                tsc[c] = s1
                P.op("dve", lambda e, c=c, s1=s1: e.tensor_tensor(out=scr[s1][:], in0=xf[:, c, :], in1=mean[:], op=ALU.subtract),
                     reads=[("xf", c), "acc1"], writes=[SC(s1)])

            def affine(c, bf):
                dst = xb[:, c, :] if bf else xf[:, c, :]
                P.op("act", lambda e, c=c, dst=dst: e.activation(out=dst, in_=xf[:, c, :], func=AF.Identity,
                                                                 bias=vec[:, b0 + c:b0 + c + 1], scale=vec[:, g0 + c:g0 + c + 1]),
                     reads=[("xf", c), "vec"], writes=[XB[c] if bf else ("xf", c)])

            AHEAD = 3
            for c in range(AHEAD):
                t_op(c)
            for c in range(8):
                s1 = tsc[c]
                P.op("dve", lambda e, c=c, s1=s1: e.tensor_tensor(out=xf[:, c, :], in0=scr[s1][:], in1=rstd[:], op=ALU.mult),
                     reads=[SC(s1), "rstd"], writes=[("xf", c)])
                if c + AHEAD < 8:
                    t_op(c + AHEAD)
                if want_bf16:
                    affine(c, True)
                else:
                    affine(c, False)
                    if after_chunk is not None:
                        after_chunk(c)
            if want_bf16:
                for c in range(8):
                    affine(c, False)
                    if after_chunk is not None:
                        after_chunk(c)

        def ffn(which, after_up=None, wide=False):
            up, dn = ("UP1", "DN1") if which == 1 else ("UP2", "DN2")
            NW = 3 if wide else 0
            wbanks = {}
            if wide:
                specs = []
                st["hold"] = st["used"]
                for j in range(NW):
                    W, wt = take_panel(up, j)
                    bg = ps_all.next()
                    bu = ps_all.next()
                    wbanks[j] = (bg, bu)
                    specs.append((bg, W, wt, 0, 256))
                    specs.append((bu, W, wt, 128, 256))
                wide_mm(specs, lambda k: xb[:, k, :], lambda k: XB[k])
                st["hold"] = None
            for j in range(NJ):
                if j < NW:
                    bg, bu = wbanks[j]
                else:
                    W, wt = take_panel(up, j)
                    bg = ps_all.next()
                    bu = ps_all.next()
                    mm_group(bg, W, 0, 256, lambda k: xb[:, k, :], 8, wt, XB)
                    mm_group(bu, W, 128, 256, lambda k: xb[:, k, :], 8, wt, XB)
                si = scr_rr.next()
                P.op("act", lambda e, bg=bg, si=si: e.activation(out=scr[si][:], in_=pbk[bg][:, :], func=AF.Silu),
                     reads=[PS(bg)], writes=[SC(si)])
                P.op("dve", lambda e, bu=bu, si=si, j=j: e.tensor_tensor(out=hT[:, j, :], in0=scr[si][:], in1=pbk[bu][:, :], op=ALU.mult),
                     reads=[SC(si), PS(bu)], writes=[HT(j)])
                if j == 3 and not st.get("xf0_loaded"):
                    st["xf0_loaded"] = True
                    load_xf(0, gate=[HT(3)])
            if after_up is not None:
                after_up()
            for n in range(8):
                W, wt = take_panel(dn, n)
                by = ps_all.next()
                mm_group(by, W, 0, 128, lambda k: hT[:, k, :], NJ, wt, [HT(k) for k in range(NJ)])
                P.op("dve", lambda e, by=by, n=n: e.scalar_tensor_tensor(out=xf[:, n, :], in0=pbk[by][:, :], scalar=C_FFN, in1=xf[:, n, :],
                                                                        op0=ALU.mult, op1=ALU.add),
                     reads=[PS(by), ("xf", n)], writes=[("xf", n)])
                ln_stats(n)

        def mixer(t):
            tok0 = t * T
            kb = {}
            specs = []
            NWK = int(os.environ.get("K_NWK", "3"))
            if NWK:
                st["hold"] = st["used"]
            for i in range(NWK):
                W, wt = take_panel("KP", i)
                for hh in range(2):
                    b = ps_all.next()
                    kb[2 * i + hh] = b
                    specs.append((b, W, wt, hh * 128, 256))
            if NWK:
                wide_mm(specs, lambda k: xb[:, k, :], lambda k: XB[k])
                st["hold"] = None
            for i in range(4):
                if i >= NWK:
                    W, wt = take_panel("KP", i)
                for hh in range(2):
                    h = 2 * i + hh
                    if i < NWK:
                        b = kb[h]
                    else:
                        b = ps_all.next()
                        mm_group(b, W, hh * 128, 256, lambda k: xb[:, k, :], 8, wt, XB)
                    P.op("act", lambda e, b=b, h=h: e.copy(out=KT[:, h, tok0:tok0 + T], in_=pbk[b][:, :]),
                         reads=[PS(b)], writes=[("KT", h, t)])
                    P.op("dve", lambda e, b=b, h=h: e.tensor_reduce(out=ksum[:, h, 2 * t:2 * t + 2],
                                                                   in_=pbk[b][:, :].rearrange("p (a b) -> p a b", a=2),
                                                                   axis=AX.X, op=ALU.add),
                         reads=[PS(b)], writes=[("ksum", h)])
                    P.op("act", lambda e, h=h: e.mul(out=kmT[:, h, 2 * t:2 * t + 2], in_=ksum[:, h, 2 * t:2 * t + 2], mul=1.0 / 256.0),
                         reads=[("ksum", h)], writes=[("kmT", h)])
            for i in range(4):
                W, wt = take_panel("QP", i)
                for hh in range(2):
                    h = 2 * i + hh
                    b = ps_all.next()
                    mm_group(b, W, hh * 128, 256, lambda k: xb[:, k, :], 8, wt, XB)
                    P.op("dve", lambda e, b=b, h=h: e.tensor_copy(out=hT[:, h, :], in_=pbk[b][:, :]), reads=[PS(b)], writes=[HT(h)])
            need_sel = t >= 2
            if need_sel:
                def gfn(e):
                    ins = None
                    for sub in range(4):
                        for h in range(NH):
                            j = sub * 8 + h
                            ins = e.matmul(pbk[7][:, j * 8:(j + 1) * 8], lhsT=hT[:, h, sub * 128:(sub + 1) * 128],
                                           rhs=kmT[:, h, :], start=True, stop=True)
                    return ins
                P.op("pe", gfn, reads=[HT(h) for h in range(NH)] + [("kmT", h) for h in range(NH)], writes=[PS(7)])
                P.op("dve", lambda e: e.tensor_tensor(out=Gm[:], in0=pbk[7][:, 0:256], in1=cstf[:, 256 + (t - 2) * 256:256 + (t - 1) * 256], op=ALU.add),
                     reads=[PS(7), "cstf"], writes=["Gm"])
                for j in range(32):
                    P.op("dve", lambda e, j=j: e.max(out=top8[:, j, :], in_=Gm[:, j * 8:(j + 1) * 8]), reads=["Gm"], writes=["top8"])
                P.op("dve", lambda e: e.tensor_tensor(out=selb[:], in0=Gm[:].rearrange("p (a b) -> p a b", b=8),
                                                      in1=top8[:, :, 2:3].to_broadcast([128, 32, 8]), op=ALU.is_ge),
                     reads=["Gm", "top8"], writes=["selb"])
            init_planes = list(range(2 * t + 1)) if need_sel else []
            init_banks = RR([6, 7])

            def make_plane0(n):
                bp = init_banks.next()

                def pfn(e, n=n, bp=bp):
                    ins = None
                    for sub in range(4):
                        ins = e.matmul(pbk[bp][:, sub * 128:(sub + 1) * 128],
                                       lhsT=selb[:, sub * 8 + 0, n:n + 1].to_broadcast([128, 128]),
                                       rhs=identb[:], start=True, stop=True)
                    return ins
                P.op("pe", pfn, reads=["selb", "identb"], writes=[PS(bp)])
                P.op("dve", lambda e, n=n, bp=bp: e.tensor_copy(out=planes[:, n, :], in_=pbk[bp][:, :]), reads=[PS(bp)], writes=[("pl", n)])
            for i in range(4):
                W, wt = take_panel("VP", i)
                for sp in range(2):
                    b = ps_all.next()

                    def fn(e, b=b, sp=sp, W=W):
                        ins = None
                        for half in range(2):
                            sub = 2 * sp + half
                            for k in range(8):
                                ins = e.matmul(pbk[b][:, half * 256:(half + 1) * 256], lhsT=xb[:, k, sub * 128:(sub + 1) * 128],
                                               rhs=W[:, k * 256:(k + 1) * 256], start=(k == 0), stop=(k == 7))
                        return ins
                    P.op("pe", fn, reads=[wt] + XB, writes=[PS(b)])
                    eng = "act" if sp == 0 else "dve"

                    def ev(e, b=b, sp=sp, i=i, eng=eng):
                        o = VV[:, 4 * t + 2 * sp:4 * t + 2 * sp + 2, i * 256:(i + 1) * 256]
                        src = pbk[b][:, :].rearrange("p (a b) -> p a b", a=2)
                        if eng == "act":
                            return e.copy(out=o, in_=src)
                        return e.tensor_copy(out=o, in_=src)
                    P.op(eng, ev, reads=[PS(b)], writes=[("VV", t, i, sp)])
                    vg = 2 * i + sp
                    if vg >= 4:
                        for _ in range(2):
                            if init_planes:
                                make_plane0(init_planes.pop(0))
            if stop == "kqv":
                return 0
            while init_planes:
                make_plane0(init_planes.pop(0))
            if stop == "kqv":
                return 0
            nkc = 4 * (t + 1)
            if need_sel:
                conv_banks = RR([7])
                plane_banks = RR([6])
            else:
                conv_banks = RR([6, 7])
                plane_banks = None

            def conv_gen():
                for c in range(8):
                    W, wt = take_panel("CP", c)
                    bh = conv_banks.next()
                    mm_group(bh, W, 0, 384, lambda k: xb[:, k, :], 8, wt, XB)
                    sh = cv_rr.next()
                    P.op("dve", lambda e, bh=bh, sh=sh: e.tensor_copy(out=cvs[sh][:], in_=pbk[bh][:, :]), reads=[PS(bh)], writes=[("cvs", sh)])
                    yield
                    bc = conv_banks.next()
                    mm_group(bc, W, 256, 384, lambda k: xb[:, k, :], 8, wt, XB)
                    ui = ub_rr.next()
                    U = ("ub", ui)
                    P.op("pool", lambda e, ui=ui, c=c: e.tensor_copy(out=ub[ui][:, 0:2], in_=uhalo[:, c, :]), reads=[("uhalo", c)], writes=[U])
                    P.op("dve", lambda e, ui=ui, sh=sh, bc=bc: e.tensor_tensor(out=ub[ui][:, 2:T + 2], in0=cvs[sh][:], in1=pbk[bc][:, :], op=ALU.mult),
                         reads=[("cvs", sh), PS(bc)], writes=[U])
                    P.op("pool", lambda e, ui=ui, c=c: e.tensor_copy(out=uhalo[:, c, :], in_=ub[ui][:, T:T + 2]), reads=[U], writes=[("uhalo", c)])
                    yield
                    bb = conv_banks.next()
                    mm_group(bb, W, 128, 384, lambda k: xb[:, k, :], 8, wt, XB)
                    sa = cv_rr.next()
                    A = ("cvs", sa)
                    P.op("dve", lambda e, ui=ui, sa=sa, c=c: e.tensor_scalar(out=cvs[sa][:], in0=ub[ui][:, 2:T + 2],
                                                                             scalar1=vec[:, V_CW[2] + c:V_CW[2] + c + 1], scalar2=vec[:, V_CB + c:V_CB + c + 1],
                                                                             op0=ALU.mult, op1=ALU.add),
                         reads=[U, "vec"], writes=[A])
                    yield
                    P.op("dve", lambda e, ui=ui, sa=sa, c=c: e.scalar_tensor_tensor(out=cvs[sa][:], in0=ub[ui][:, 1:T + 1], scalar=vec[:, V_CW[1] + c:V_CW[1] + c + 1],
                                                                                    in1=cvs[sa][:], op0=ALU.mult, op1=ALU.add),
                         reads=[U, A, "vec"], writes=[A])
                    P.op("dve", lambda e, ui=ui, sa=sa, c=c: e.scalar_tensor_tensor(out=cvs[sa][:], in0=ub[ui][:, 0:T], scalar=vec[:, V_CW[0] + c:V_CW[0] + c + 1],
                                                                                    in1=cvs[sa][:], op0=ALU.mult, op1=ALU.add),
                         reads=[U, A, "vec"], writes=[A])
                    yield
                    P.op("dve", lambda e, sa=sa, bb=bb, c=c: e.tensor_tensor(out=hT[:, 16 + c, :], in0=cvs[sa][:], in1=pbk[bb][:, :], op=ALU.mult),
                         reads=[A, PS(bb)], writes=[HT(16 + c)])
                    yield

            def make_plane(h, n):
                bp = plane_banks.next()

                def pfn(e, h=h, n=n, bp=bp):
                    ins = None
                    for sub in range(4):
                        ins = e.matmul(pbk[bp][:, sub * 128:(sub + 1) * 128],
                                       lhsT=selb[:, sub * 8 + h, n:n + 1].to_broadcast([128, 128]),
                                       rhs=identb[:], start=True, stop=True)
                    return ins
                P.op("pe", pfn, reads=["selb", "identb"], writes=[PS(bp)])
                P.op("dve", lambda e, n=n, bp=bp: e.tensor_copy(out=planes[:, n, :], in_=pbk[bp][:, :]), reads=[PS(bp)], writes=[("pl", n)])

            hb = {}

            def banks_of(h):
                if h not in hb:
                    hb[h] = (ps_O.next(), ps_D.next())
                return hb[h]

            def emit_S(h, kc):
                tt = kc // 4
                q0 = (kc - 4 * t) * 128 if tt == t else 0
                bs = ps_S.next()
                pi = pt_rr.next()
                P.op("pe", lambda e: e.matmul(pbk[bs][:, q0:T], lhsT=KT[:, h, kc * 128:(kc + 1) * 128],
                                              rhs=hT[:, h, q0:T], start=True, stop=True),
                     reads=[("KT", h, tt), HT(h)], writes=[PS(bs)])
                P.op("act", lambda e: e.activation(out=PT[pi][:, q0:T], in_=pbk[bs][:, q0:T], func=AF.Exp, scale=SCALE),
                     reads=[PS(bs)], writes=[("PT", pi)])
                n = kc // 2
                if tt == t:
                    P.op("dve", lambda e: e.tensor_tensor(out=PT[pi][:, q0:q0 + 128], in0=PT[pi][:, q0:q0 + 128], in1=trib[:], op=ALU.mult),
                         reads=[("PT", pi), "trib"], writes=[("PT", pi)])
                    if need_sel and n == 2 * t:
                        P.op("dve", lambda e: e.tensor_tensor(out=PT[pi][:, 256:T], in0=PT[pi][:, 256:T], in1=planes[:, n, 256:T], op=ALU.mult),
                             reads=[("PT", pi), ("pl", n)], writes=[("PT", pi)])
                elif need_sel:
                    P.op("dve", lambda e: e.tensor_tensor(out=PT[pi][:, :], in0=PT[pi][:, :], in1=planes[:, n, :], op=ALU.mult),
                         reads=[("PT", pi), ("pl", n)], writes=[("PT", pi)])
                if need_sel and h + 1 < NH and n <= 2 * t and kc == 2 * n + 1:
                    make_plane(h + 1, n)
                return pi, q0, tt

            def emit_PV(h, kc, pi, q0, tt):
                bo, bd = banks_of(h)
                first = kc == 0
                last = kc == nkc - 1
                sub = kc % 4

                def fn(e):
                    e.matmul(pbk[bo][:, q0:T], lhsT=VV[:, kc, h * 128:(h + 1) * 128], rhs=PT[pi][:, q0:T],
                             start=first, stop=last, skip_group_check=True)
                    return e.matmul(pbk[bd][:, q0:T], lhsT=onesb[:], rhs=PT[pi][:, q0:T],
                                    start=first, stop=last, skip_group_check=True)
                P.op("pe", fn, reads=[("VV", tt, h // 2, sub // 2), ("PT", pi), "onesb"], writes=[PS(bo), PS(bd)])
                if last:
                    si = scr_rr.next()
                    P.op("act", lambda e: e.activation(out=scr[si][:], in_=pbk[bd][:, :], func=AF.Ln), reads=[PS(bd)], writes=[SC(si)])
                    P.op("act", lambda e: e.activation(out=scr[si][:], in_=scr[si][:], func=AF.Exp, scale=-1.0), reads=[SC(si)], writes=[SC(si)])
                    P.op("dve", lambda e: e.tensor_tensor(out=hT[:, 8 + h, :], in0=pbk[bo][:, :], in1=scr[si][:], op=ALU.mult),
                         reads=[PS(bo), SC(si)], writes=[HT(8 + h)])

            jobs = [(h, kc) for h in range(NH) for kc in range(nkc)]
            conv_it = conv_gen()
            NPIECE = 8 * 5
            every = max(1, len(jobs) // NPIECE)
            LOOK = 4
            sres = {}
            for step in range(len(jobs) + LOOK):
                if step < len(jobs):
                    sres[step] = emit_S(*jobs[step])
                if step >= LOOK:
                    j = step - LOOK
                    emit_PV(*jobs[j], *sres[j])
                if step % every == 0:
                    next(conv_it, None)
            for _ in conv_it:
                pass
            if stop == "attn":
                return 8
            if stop == "conv":
                return 16
            for c in range(8):
                Wg, wtg = take_panel("GA", c)
                bga = ps_all.next()
                bgc = ps_all.next()
                mm_group(bga, Wg, 0, 256, lambda k: xb[:, k, :], 8, wtg, XB)
                mm_group(bgc, Wg, 128, 256, lambda k: xb[:, k, :], 8, wtg, XB)
                Wp, wtp = take_panel("PP", c)
                bya = ps_all.next()
                byc = ps_all.next()
                mm_group(bya, Wp, 0, 256, lambda k: hT[:, 8 + k, :], 8, wtp, [HT(8 + k) for k in range(8)])
                mm_group(byc, Wp, 128, 256, lambda k: hT[:, 16 + k, :], 8, wtp, [HT(16 + k) for k in range(8)])
                s1 = scr_rr.next()
                s2 = scr_rr.next()
                P.op("act", lambda e, bga=bga, s1=s1, c=c: e.activation(out=scr[s1][:], in_=pbk[bga][:, :], func=AF.Sigmoid,
                                                                        bias=vec[:, V_BG + c:V_BG + c + 1], scale=1.0),
                     reads=[PS(bga), "vec"], writes=[SC(s1)])
                P.op("act", lambda e, bgc=bgc, s2=s2, c=c: e.activation(out=scr[s2][:], in_=pbk[bgc][:, :], func=AF.Sigmoid,
                                                                        bias=vec[:, V_BG + 8 + c:V_BG + 8 + c + 1], scale=1.0),
                     reads=[PS(bgc), "vec"], writes=[SC(s2)])
                P.op("dve", lambda e, bya=bya, s1=s1: e.tensor_tensor(out=scr[s1][:], in0=scr[s1][:], in1=pbk[bya][:, :], op=ALU.mult),
                     reads=[SC(s1), PS(bya)], writes=[SC(s1)])
                P.op("dve", lambda e, byc=byc, s2=s2: e.tensor_tensor(out=scr[s2][:], in0=scr[s2][:], in1=pbk[byc][:, :], op=ALU.mult),
                     reads=[SC(s2), PS(byc)], writes=[SC(s2)])
                P.op("dve", lambda e, s1=s1, s2=s2, c=c: e.tensor_tensor(out=hT[:, c, :], in0=scr[s1][:], in1=scr[s2][:], op=ALU.add),
                     reads=[SC(s1), SC(s2)], writes=[HT(c)])
            if stop == "merge":
                return 0
            for i in range(4):
                W, wt = take_panel("WO", i)
                for hh in range(2):
                    n = 2 * i + hh
                    b = ps_all.next()
                    mm_group(b, W, hh * 128, 256, lambda k: hT[:, k, :], 8, wt, [HT(k) for k in range(8)])
                    P.op("dve", lambda e, b=b, n=n: e.scalar_tensor_tensor(out=xf[:, n, :], in0=pbk[b][:, :], scalar=C_MIX, in1=xf[:, n, :],
                                                                          op0=ALU.mult, op1=ALU.add),
                         reads=[PS(b), ("xf", n)], writes=[("xf", n)])
                    ln_stats(n)

        def load_xb(t):
            P.dma("pool", "xbl", lambda e: e.dma_start(out=xb[:].rearrange("p a b -> p (a b)"), in_=xT[t]), writes=XB)

        def load_xf(t, gate=()):
            P.dma("sp", "xfl", lambda e: e.dma_start(out=xf[:].rearrange("p a b -> p (a b)"), in_=xT[t]), reads=list(gate),
                  writes=[("xf", c) for c in range(8)])

        load_xb(0)

        def dump(t, from_hT=None):
            for c in range(8):
                if from_hT is not None:
                    P.op("dve", lambda e, c=c: e.tensor_copy(out=xf[:, c, :], in_=hT[:, from_hT + c, :]),
                         reads=[HT(from_hT + c)], writes=[("xf", c)])
                P.dma("sp", "st%d" % c, lambda e, c=c: e.dma_start(out=oTv[:, c, t * T:(t + 1) * T], in_=xf[:, c, :]), reads=[("xf", c)])

        for t in range(ntiles):
            last_t = t + 1 == ntiles
            ffn(1)
            if stop == "ffn1" and last_t:
                dump(t)
                break
            layer_norm(0, True)
            if stop == "ln1" and last_t:
                dump(t)
                break
            r = mixer(t)
            if r is not None:
                if last_t:
                    dump(t, from_hT=r)
                    break
                else:
                    raise ValueError("mixer stop only with single tile")
            if stop == "wout" and last_t:
                dump(t)
                break
            layer_norm(1, True)
            if stop == "ln2" and last_t:
                dump(t)
                break
            ffn(2, after_up=(partial(load_xb, t + 1) if t + 1 < ntiles else None), wide=True)

            def store(c, t=t):
                P.dma("sp", "st%d" % c, lambda e: e.dma_start(out=oTv[:, c, t * T:(t + 1) * T], in_=xf[:, c, :]), reads=[("xf", c)])
            layer_norm(2, False, after_chunk=store)
            if t + 1 < ntiles:
                load_xf(t + 1)
        P.final_wait("sp", ["st%d" % c for c in range(8)])
        P.emit(block)
        print("ops:", {k: len(v) for k, v in P.streams.items()})
    return nc


_CACHE = {}


def kernel(**inputs):
    x = np.asarray(inputs["x"], np.float32)
    B = x.shape[0]
    wp = pack_weights(inputs)
    vec = pack_vec(inputs)
    cst = pack_cst()
    if "nc" not in _CACHE:
        _CACHE["nc"] = build_nc()
    nc = _CACHE["nc"]
    in_maps = []
    for b in range(B):
        xt = np.ascontiguousarray(x[b].reshape(NT, T, 8, 128).transpose(0, 3, 2, 1)).reshape(NT, 128, 8 * T)
        in_maps.append({"xT": xt, "wpk": wp, "vec": vec, "cst": cst})
    res = run_bass_kernel_spmd(nc, in_maps, core_ids=list(range(B)))
    out = np.stack([np.ascontiguousarray(r["outT"].T) for r in res.results], axis=0)
    return out.astype(np.float32)
```
